# Optimizing a Trainium2 kernel written in Bass

```python
import math
import jax, jax.numpy as jnp
from jax import lax
import numpy as np

D_MODEL = 4096
BATCH = 1
SEQ = 16384
DEPTH = 4

CHUNK = 64
N_META = 16
N_MIXERS = 3
NORM_EPS = 1e-6
N_LAYERS_A = (DEPTH + 2) // 3
N_LAYERS_B = (DEPTH + 1) // 3
N_LAYERS_C = DEPTH // 3

A_HEAD_DIM = 128
A_HEADS = D_MODEL // A_HEAD_DIM

B_QK_DIM = 128
B_HEADS = D_MODEL // (2 * B_QK_DIM)
B_V_DIM = 2 * B_QK_DIM
ROPE_THETA = 500000.0
ROPE_DIM = B_QK_DIM // 4
Q_BLOCK = 128
SUBLN_EPS = 1e-5

C_HEAD_DIM = 64
C_HEADS = D_MODEL // C_HEAD_DIM
C_DECAY_LORA = 128
C_AAA_LORA = 128
C_DECAY_SCALE = 0.606531
LNX_EPS = 64e-5

kernel_name = 'hybrid_hgrn2_diffattn_rwkv7_trunk'

F32 = jnp.float32


def rmsnorm(x, g, eps=NORM_EPS):
    xf = x.astype(F32)
    y = xf * lax.rsqrt(jnp.mean(xf * xf, axis=-1, keepdims=True) + eps)
    return (y * g.astype(F32)).astype(x.dtype)


def chunk_id(pos):
    return jnp.where(pos < N_META, 0, 1 + (pos - N_META) // CHUNK)


def partial_rope(x, cos, sin):
    half = ROPE_DIM // 2
    x1, x2, rest = x[..., :half], x[..., half:ROPE_DIM], x[..., ROPE_DIM:]
    return jnp.concatenate([(x1 * cos - x2 * sin).astype(x.dtype),
                            (x2 * cos + x1 * sin).astype(x.dtype), rest], axis=-1)


def hgrn2_mixer(u, w_in, lb, gnorm_g, w_out):
    bsz, seqlen, d = u.shape
    q, f_logit, i_in, gate = jnp.split(u @ w_in, 4, axis=-1)
    f = lb + (1.0 - lb) * jax.nn.sigmoid(f_logit.astype(F32))
    log_f = jnp.log(f)
    k = 1.0 - f
    pad = (-seqlen) % CHUNK
    n_chunks = (seqlen + pad) // CHUNK

    def to_chunks(t):
        t = jnp.pad(t.astype(F32), ((0, 0), (pad, 0), (0, 0)))
        return t.reshape(bsz, n_chunks, CHUNK, A_HEADS, A_HEAD_DIM).transpose(1, 0, 3, 2, 4)

    qc, kc, vc, gc = (to_chunks(t) for t in (q, k, i_in, log_f))
    causal = jnp.tril(jnp.ones((CHUNK, CHUNK), dtype=bool))[:, :, None]

    def step(state, inp):
        qb, kb, vb, gb = inp
        b = jnp.cumsum(gb, axis=2)
        b_last = b[:, :, -1:, :]
        rel = jnp.where(causal, b[:, :, :, None, :] - b[:, :, None, :, :], -jnp.inf)
        scores = jnp.einsum('bhtd,bhsd,bhtsd->bhts', qb, kb, jnp.exp(rel))
        out = (jnp.einsum('bhts,bhse->bhte', scores, vb)
               + jnp.einsum('bhtd,bhde->bhte', qb * jnp.exp(b), state))
        state = (jnp.exp(b_last[:, :, 0, :, None]) * state
                 + jnp.einsum('bhsd,bhse->bhde', kb * jnp.exp(b_last - b), vb))
        return state, out

    s0 = jnp.zeros((bsz, A_HEADS, A_HEAD_DIM, A_HEAD_DIM), F32)
    _, o = lax.scan(step, s0, (qc, kc, vc, gc))
    o = o.transpose(1, 0, 3, 2, 4).reshape(bsz, n_chunks * CHUNK, A_HEADS, A_HEAD_DIM)[:, pad:]
    o = rmsnorm(o, gnorm_g).reshape(bsz, seqlen, d)
    return (o * jax.nn.silu(gate.astype(F32))).astype(u.dtype) @ w_out


def diff_attn_mixer(u, w_in, lam_q1, lam_k1, lam_q2, lam_k2, subln_g, w_out,
                    lambda_init, cos, sin, kcid):
    bsz, seqlen, d = u.shape
    q, k, v, gate = jnp.split(u @ w_in, 4, axis=-1)
    q = partial_rope(q.reshape(bsz, seqlen, B_HEADS, 2, B_QK_DIM), cos, sin)
    k = partial_rope(k.reshape(bsz, seqlen, B_HEADS, 2, B_QK_DIM), cos, sin)
    k = k.transpose(0, 2, 3, 1, 4)
    v = v.reshape(bsz, seqlen, B_HEADS, B_V_DIM).transpose(0, 2, 1, 3)
    lam = (jnp.exp(jnp.dot(lam_q1.astype(F32), lam_k1.astype(F32)))
           - jnp.exp(jnp.dot(lam_q2.astype(F32), lam_k2.astype(F32))) + lambda_init)
    n_blocks = -(-seqlen // Q_BLOCK)
    q_len = n_blocks * Q_BLOCK
    q = jnp.pad(q, ((0, 0), (0, q_len - seqlen), (0, 0), (0, 0), (0, 0)))
    q = q.reshape(bsz, n_blocks, Q_BLOCK, B_HEADS, 2, B_QK_DIM).transpose(1, 0, 3, 4, 2, 5)
    qcid = chunk_id(jnp.arange(q_len)).reshape(n_blocks, Q_BLOCK)
    scale = B_QK_DIM ** -0.5

    def attend(args):
        qb, qc = args
        s = jnp.einsum('bhcqd,bhckd->bhcqk', qb, k).astype(F32) * scale
        s = jnp.where(kcid[None, :] <= qc[:, None], s, -jnp.inf)
        p = jax.nn.softmax(s, axis=-1)
        a = p[:, :, 0] - lam * p[:, :, 1]
        return jnp.einsum('bhqk,bhke->bhqe', a.astype(v.dtype), v)

    o = lax.map(attend, (q, qcid))
    o = o.transpose(1, 0, 3, 2, 4).reshape(bsz, q_len, B_HEADS, B_V_DIM)[:, :seqlen]
    o = (rmsnorm(o, subln_g, SUBLN_EPS) * (1.0 - lambda_init)).reshape(bsz, seqlen, d)
    return (o * jax.nn.silu(gate)).astype(u.dtype) @ w_out


def rwkv7_mixer(u, mu, w_in, w0, w1, w2, a0, a1, a2, k_k, k_a, r_k, lnx_w, lnx_b, w_out):
    bsz, seqlen, d = u.shape
    delta = jnp.pad(u, ((0, 0), (1, 0), (0, 0)))[:, :-1] - u

    def mixed(s):
        return u + delta * mu[s]

    r, k, v, gate = (mixed(s) @ w_in[:, s * d:(s + 1) * d] for s in range(4))
    decay = jnp.exp(-C_DECAY_SCALE * jax.nn.sigmoid((w0 + jnp.tanh(mixed(4) @ w1) @ w2).astype(F32)))
    a = jax.nn.sigmoid((a0 + (mixed(5) @ a1) @ a2).astype(F32))

    def heads(t):
        return t.astype(F32).reshape(bsz, seqlen, C_HEADS, C_HEAD_DIM)

    r, k, v, decay, a = (heads(t) for t in (r, k, v, decay, a))
    kk = k * k_k.astype(F32).reshape(C_HEADS, C_HEAD_DIM)
    kk = kk / jnp.maximum(jnp.sqrt(jnp.sum(kk * kk, axis=-1, keepdims=True)), 1e-12)
    k = k * (1.0 + (a - 1.0) * k_a.astype(F32).reshape(C_HEADS, C_HEAD_DIM))

    def step(state, inp):
        r_t, w_t, k_t, v_t, kk_t, a_t = inp
        sa = jnp.einsum('bhij,bhj->bhi', state, -kk_t)
        state = (state * w_t[:, :, None, :] + sa[..., None] * (kk_t * a_t)[:, :, None, :]
                 + v_t[..., None] * k_t[:, :, None, :])
        return state, jnp.einsum('bhij,bhj->bhi', state, r_t)

    def seq_first(t):
        return t.transpose(1, 0, 2, 3)

    s0 = jnp.zeros((bsz, C_HEADS, C_HEAD_DIM, C_HEAD_DIM), F32)
    _, o = lax.scan(step, s0, tuple(seq_first(t) for t in (r, decay, k, v, kk, a)))
    o = seq_first(o)
    mean = jnp.mean(o, axis=-1, keepdims=True)
    var = jnp.mean(jnp.square(o - mean), axis=-1, keepdims=True)
    o = ((o - mean) * lax.rsqrt(var + LNX_EPS)).reshape(bsz, seqlen, d) * lnx_w.astype(F32) + lnx_b.astype(F32)
    bonus = jnp.sum(r * k * r_k.astype(F32), axis=-1, keepdims=True) * v
    o = o + bonus.reshape(bsz, seqlen, d)
    return (o * jax.nn.silu(gate.astype(F32))).astype(u.dtype) @ w_out


def setup_inputs(seed: int = 0) -> dict:
    key = jax.random.key(seed)
    ks = iter(jax.random.split(key, 32))
    d = D_MODEL
    s = d ** -0.5

    def nrm(shape, scale):
        return jax.random.normal(next(ks), shape, F32) * scale

    inp = {}
    inp['x'] = nrm((BATCH, SEQ, d), 1.0)
    inp['meta_tokens'] = nrm((N_META, d), 1.0)
    inp['pre_norm_g'] = 1.0 + nrm((DEPTH, d), 0.02)
    inp['post_norm_g'] = 1.0 + nrm((DEPTH, d), 0.02)
    inp['a_w_in'] = nrm((N_LAYERS_A, d, 4 * d), s)
    inp['a_lb_logits'] = nrm((N_LAYERS_A, d), 0.5)
    inp['a_gnorm_g'] = 1.0 + nrm((N_LAYERS_A, A_HEAD_DIM), 0.02)
    inp['a_w_out'] = nrm((N_LAYERS_A, d, d), s)
    inp['b_w_in'] = nrm((N_LAYERS_B, d, 4 * d), s)
    inp['b_lam_q1'] = nrm((N_LAYERS_B, B_QK_DIM), 0.1)
    inp['b_lam_k1'] = nrm((N_LAYERS_B, B_QK_DIM), 0.1)
    inp['b_lam_q2'] = nrm((N_LAYERS_B, B_QK_DIM), 0.1)
    inp['b_lam_k2'] = nrm((N_LAYERS_B, B_QK_DIM), 0.1)
    inp['b_subln_g'] = 1.0 + nrm((N_LAYERS_B, B_V_DIM), 0.02)
    inp['b_w_out'] = nrm((N_LAYERS_B, d, d), s)
    inp['c_mu'] = jax.random.uniform(next(ks), (N_LAYERS_C, 6, d), F32)
    inp['c_w_in'] = nrm((N_LAYERS_C, d, 4 * d), s)
    inp['c_w0'] = nrm((N_LAYERS_C, d), 1.0)
    inp['c_w1'] = nrm((N_LAYERS_C, d, C_DECAY_LORA), s)
    inp['c_w2'] = nrm((N_LAYERS_C, C_DECAY_LORA, d), 0.1 * C_DECAY_LORA ** -0.5)
    inp['c_a0'] = nrm((N_LAYERS_C, d), 0.1)
    inp['c_a1'] = nrm((N_LAYERS_C, d, C_AAA_LORA), s)
    inp['c_a2'] = nrm((N_LAYERS_C, C_AAA_LORA, d), 0.1 * C_AAA_LORA ** -0.5)
    inp['c_k_k'] = 0.85 + nrm((N_LAYERS_C, d), 0.02)
    inp['c_k_a'] = 1.0 + nrm((N_LAYERS_C, d), 0.02)
    inp['c_r_k'] = nrm((N_LAYERS_C, C_HEADS, C_HEAD_DIM), 0.1)
    inp['c_lnx_w'] = 1.0 + nrm((N_LAYERS_C, d), 0.02)
    inp['c_lnx_b'] = nrm((N_LAYERS_C, d), 0.02)
    inp['c_w_out'] = nrm((N_LAYERS_C, d, d), s)
    return inp


def reference(x, meta_tokens, pre_norm_g, post_norm_g,
              a_w_in, a_lb_logits, a_gnorm_g, a_w_out,
              b_w_in, b_lam_q1, b_lam_k1, b_lam_q2, b_lam_k2, b_subln_g, b_w_out,
              c_mu, c_w_in, c_w0, c_w1, c_w2, c_a0, c_a1, c_a2, c_k_k, c_k_a, c_r_k,
              c_lnx_w, c_lnx_b, c_w_out):
    bsz = x.shape[0]
    meta = jnp.broadcast_to(meta_tokens.astype(x.dtype)[None], (bsz, N_META, D_MODEL))
    h = jnp.concatenate([meta, x], axis=1)
    seqlen = h.shape[1]
    pos = jnp.arange(seqlen)
    kcid = chunk_id(pos)
    inv_freq = ROPE_THETA ** (-jnp.arange(0, ROPE_DIM, 2, dtype=F32) / ROPE_DIM)
    ang = pos.astype(F32)[:, None] * inv_freq[None, :]
    cos = jnp.cos(ang)[:, None, None, :]
    sin = jnp.sin(ang)[:, None, None, :]
    lb_p = jax.nn.softmax(a_lb_logits.astype(F32), axis=0)
    lb_all = jnp.cumsum(lb_p, axis=0) - lb_p[0]

    for i in range(DEPTH):
        kind, j = i % N_MIXERS, i // N_MIXERS
        u = rmsnorm(h, pre_norm_g[i])
        if kind == 0:
            y = hgrn2_mixer(u, a_w_in[j], lb_all[j], a_gnorm_g[j], a_w_out[j])
        elif kind == 1:
            lambda_init = 0.8 - 0.6 * math.exp(-0.3 * i)
            y = diff_attn_mixer(u, b_w_in[j], b_lam_q1[j], b_lam_k1[j], b_lam_q2[j], b_lam_k2[j],
                                b_subln_g[j], b_w_out[j], lambda_init, cos, sin, kcid)
        else:
            y = rwkv7_mixer(u, c_mu[j], c_w_in[j], c_w0[j], c_w1[j], c_w2[j], c_a0[j], c_a1[j], c_a2[j],
                            c_k_k[j], c_k_a[j], c_r_k[j], c_lnx_w[j], c_lnx_b[j], c_w_out[j])
        h = h + rmsnorm(y, post_norm_g[i])

    return h[:, N_META:]
```

```python
import numpy as np
import concourse.bass as bass
import concourse.mybir as mybir
from concourse.bass_utils import run_bass_kernel_spmd

F32 = mybir.dt.float32
BF16 = mybir.dt.bfloat16
ALU = mybir.AluOpType
AF = mybir.ActivationFunctionType
AX = mybir.AxisListType


class Sched:
    def __init__(self, nc, n_dma_sems=24):
        self.nc = nc
        self.eng = {'pe': nc.tensor, 'act': nc.scalar, 'dve': nc.vector,
                    'pool': nc.gpsimd, 'sp': nc.sync}
        self.prog = {}
        self.cnt = {}
        self.known = {e: {} for e in self.eng}
        self.same_sync = {'pe': False, 'act': True, 'dve': True, 'pool': True, 'sp': True}
        self.res = {}
        self._ctx = []
        for e in ('pe', 'act', 'dve', 'pool'):
            cm = nc.semaphore('prog_' + e)
            self.prog[e] = cm.__enter__()
            self._ctx.append(cm)
            self.cnt[e] = 0
        self.dsem = []
        self.dcnt = []
        for i in range(n_dma_sems):
            cm = nc.semaphore('dma_%d' % i)
            self.dsem.append(cm.__enter__())
            self._ctx.append(cm)
            self.dcnt.append(0)
        self.drr = 0
        self.sem_name = {}
        self.n_wait = 0

    def close(self):
        for cm in reversed(self._ctx):
            cm.__exit__(None, None, None)

    def _wait(self, e, tok):
        sem, val = tok
        k = self.known[e]
        key = id(sem)
        if k.get(key, 0) >= val:
            return
        self.eng[e].wait_ge(sem, val)
        self.n_wait += 1
        k[key] = val

    @staticmethod
    def _is_excl(k):
        if isinstance(k, tuple):
            k = k[0]
        return isinstance(k, str) and k.startswith('p_')

    def _split(self, reads, writes):
        reads = list(reads)
        writes = list(writes)
        ex = [k for k in reads if self._is_excl(k)]
        if ex:
            reads = [k for k in reads if not self._is_excl(k)]
            writes = writes + [k for k in ex if k not in writes]
        return reads, writes

    def _deps(self, reads, writes):
        toks = {}

        def add(t):
            if t is None:
                return
            key = id(t[0])
            if key not in toks or toks[key][1] < t[1]:
                toks[key] = t
        for k in reads:
            r = self.res.get(k)
            if r:
                add(r['w'])
        for k in writes:
            r = self.res.get(k)
            if r:
                add(r['w'])
                for t in r['r'].values():
                    add(t)
        return list(toks.values())

    def _update(self, tok, reads, writes):
        for k in reads:
            r = self.res.setdefault(k, {'w': None, 'r': {}})
            key = id(tok[0])
            if key not in r['r'] or r['r'][key][1] < tok[1]:
                r['r'][key] = tok
        for k in writes:
            self.res[k] = {'w': tok, 'r': {}}

    def op(self, e, fn, reads=(), writes=()):
        reads, writes = self._split(reads, writes)
        for t in self._deps(reads, writes):
            if t[0] is self.prog.get(e) and not self.same_sync[e]:
                continue
            self._wait(e, t)
        inst = fn(self.eng[e])
        self.cnt[e] += 1
        inst.then_inc(self.prog[e], 1)
        tok = (self.prog[e], self.cnt[e])
        self._update(tok, reads, writes)
        return tok

    def dma(self, q, out, in_, reads=(), writes=(), **kw):
        reads, writes = self._split(reads, writes)
        for t in self._deps(reads, writes):
            self._wait(q, t)
        s = self.drr
        self.drr = (self.drr + 1) % len(self.dsem)
        if self.dcnt[s] > 0:
            self._wait(q, (self.dsem[s], 16 * self.dcnt[s]))
        inst = self.eng[q].dma_start(out=out, in_=in_, **kw)
        self.dcnt[s] += 1
        inst.then_inc(self.dsem[s], 16)
        tok = (self.dsem[s], 16 * self.dcnt[s])
        self._update(tok, reads, writes)
        return tok

    def finish(self, e='sp'):
        for s in range(len(self.dsem)):
            if self.dcnt[s] > 0:
                self._wait(e, (self.dsem[s], 16 * self.dcnt[s]))
        for x in ('pe', 'act', 'dve', 'pool'):
            if self.cnt[x] > 0:
                self._wait(e, (self.prog[x], self.cnt[x]))


class Alloc:
    _uid = [0]

    def __init__(self, nc):
        self.nc = nc
        self._ctx = []
        Alloc._uid[0] += 1
        self.sfx = "_a%d" % Alloc._uid[0]

    def sb(self, name, shape, dt):
        cm = self.nc.sbuf_tensor(name + self.sfx, shape, dt)
        t = cm.__enter__()
        self._ctx.append(cm)
        return t

    def ps(self, name, shape, dt):
        cm = self.nc.psum_tensor(name + self.sfx, shape, dt)
        t = cm.__enter__()
        self._ctx.append(cm)
        return t

    def close(self):
        for cm in reversed(self._ctx):
            cm.__exit__(None, None, None)
        self._ctx = []


def barrier(S):
    toks = []
    for s in range(len(S.dsem)):
        if S.dcnt[s] > 0:
            toks.append((S.dsem[s], 16 * S.dcnt[s]))
    for x in ('pe', 'act', 'dve', 'pool'):
        if S.cnt[x] > 0:
            toks.append((S.prog[x], S.cnt[x]))
    for e in ('pe', 'act', 'dve', 'pool', 'sp'):
        for t in toks:
            S._wait(e, t)
    S.res = {}


def emit_prep(S, nc, *, Lp, D, h, g, uT_scr, ident, eps=1e-6, norm=True):
    KC = D // 128
    NTILE = Lp // 128
    A = Alloc(nc)
    hb = [A.sb("hb%d" % i, [128, D], F32) for i in range(2)]
    ub = [A.sb("ub%d" % i, [128, D], BF16) for i in range(2)]
    junk = A.sb("junk", [128, D], BF16)
    gb = A.sb("gb", [128, D], F32)
    ss = [A.sb("ss%d" % i, [128, 1], F32) for i in range(2)]
    rs = [A.sb("rs%d" % i, [128, 1], F32) for i in range(2)]
    idf = A.sb("idf", [128, 128], F32)
    idb = A.sb("idb", [128, 128], BF16)
    uo = [A.sb("uo%d" % i, [128, KC, 128], BF16) for i in range(2)]
    zc = A.sb("zc", [128, KC, 1], BF16)
    G = min(8, KC)
    pT = [A.ps("pT%d" % i, [128, G, 128], BF16) for i in range(2)]
    S.dma('sp', idf[:], ident[:, :], writes=['idf'])
    S.op('dve', lambda e: e.tensor_copy(out=idb[:], in_=idf[:]), reads=['idf'], writes=['idb'])
    if norm:
        S.dma('sp', gb[:], g[0:1, :].partition_broadcast(128), writes=['gb'])
    S.op('pool', lambda e: e.memset(zc[:], 0.0), writes=['zc'])
    S.dma('sp', uT_scr[:, :, 0:1], zc[:], reads=['zc'], allow_slow_non_contiguous=True)
    ev = 0
    for j in range(NTILE):
        sl = j % 2
        S.dma('sp', hb[sl][:], h[j * 128:(j + 1) * 128, :], writes=['hb%d' % sl])
        if norm:
            S.op('dve', lambda e: e.memset(ss[sl][:], 0.0), writes=['ss%d' % sl])
            S.op('act', lambda e: e.activation(out=junk[:], in_=hb[sl][:], func=AF.Square, accum_out=ss[sl][:]),
                 reads=['hb%d' % sl], writes=['junk', 'ss%d' % sl])
            S.op('dve', lambda e: e.tensor_scalar(out=rs[sl][:], in0=ss[sl][:], scalar1=1.0 / D, scalar2=eps,
                                                  op0=ALU.mult, op1=ALU.add), reads=['ss%d' % sl], writes=['rs%d' % sl])
            S.op('act', lambda e: e.sqrt(out=rs[sl][:], in_=rs[sl][:]), reads=['rs%d' % sl], writes=['rs%d' % sl])
            S.op('dve', lambda e: e.reciprocal(out=rs[sl][:], in_=rs[sl][:]), reads=['rs%d' % sl], writes=['rs%d' % sl])
            S.op('dve', lambda e: e.scalar_tensor_tensor(out=ub[sl][:], in0=hb[sl][:], scalar=rs[sl][:, 0:1], in1=gb[:],
                                                         op0=ALU.mult, op1=ALU.mult),
                 reads=['hb%d' % sl, 'rs%d' % sl, 'gb'], writes=['ub%d' % sl])
        else:
            S.op('dve', lambda e: e.tensor_copy(out=ub[sl][:], in_=hb[sl][:]), reads=['hb%d' % sl], writes=['ub%d' % sl])
        for k0 in range(0, KC, G):
            pb = ev % 2
            for k in range(k0, k0 + G):
                S.op('pe', lambda e, k=k: e.transpose(out=pT[pb][:, k - k0, :], in_=ub[sl][:, k * 128:(k + 1) * 128],
                                                      identity=idb[:]),
                     reads=['ub%d' % sl, 'idb'], writes=['p_T%d' % pb])
            eng = 'act' if ev % 2 == 0 else 'dve'
            if eng == 'act':
                S.op('act', lambda e: e.copy(out=uo[sl][:, k0:k0 + G, :], in_=pT[pb][:]),
                     reads=['p_T%d' % pb], writes=[('uo', sl, k0)])
            else:
                S.op('dve', lambda e: e.tensor_copy(out=uo[sl][:, k0:k0 + G, :], in_=pT[pb][:]),
                     reads=['p_T%d' % pb], writes=[('uo', sl, k0)])
            ev += 1
        S.dma('act', uT_scr[:, :, 1 + j * 128:1 + (j + 1) * 128], uo[sl][:],
              reads=[('uo', sl, k0) for k0 in range(0, KC, G)], writes=['uT_scr'])
    barrier(S)
    A.close()


def emit_mm_pass(S, nc, *, Lp, D, uT_scr, W, ncols, tiles, zF, zT, mu=None, nvar=1, TB=512):
    KC = D // 128
    A = Alloc(nc)
    Wb = A.sb("Wb", [128, KC, ncols], BF16)
    SW = min(ncols, 512)
    Wst = [A.sb("Wst%d" % i, [128, SW], F32) for i in range(2)]
    use_var = mu is not None
    NUB = 1 if use_var else 2
    uT = [A.sb("uT%d" % i, [128, KC, TB + 1], BF16) for i in range(NUB)]
    st = [A.sb("st%d" % i, [128, 512], F32) for i in range(4)]
    pz = [A.ps("pz%d" % i, [128, 512], F32) for i in range(4)]
    if use_var:
        muT = A.sb("muT", [128, nvar, KC], F32)
        dT = A.sb("dT", [128, KC, TB], BF16)
        mx = [A.sb("mx%d" % i, [128, KC, TB], BF16) for i in range(2)]
        S.dma('sp', muT[:], mu.rearrange("v (k p) -> p v k", p=128), writes=['muT'], allow_slow_non_contiguous=True)
    i = 0
    for k in range(KC):
        for c0 in range(0, ncols, SW):
            sl = i % 2
            S.dma('sp', Wst[sl][:, 0:min(SW, ncols - c0)], W[k * 128:(k + 1) * 128, c0:c0 + min(SW, ncols - c0)],
                  writes=['Wst%d' % sl])
            S.op('pool', lambda e, k=k, c0=c0, sl=sl: e.tensor_copy(out=Wb[:, k, c0:c0 + min(SW, ncols - c0)],
                                                                   in_=Wst[sl][:, 0:min(SW, ncols - c0)]),
                 reads=['Wst%d' % sl], writes=[('Wb', k)])
            i += 1
    Wkeys = [('Wb', k) for k in range(KC)]
    nblk = (Lp + TB - 1) // TB
    oc = 0
    vars_used = sorted(set(t[1] for t in tiles))
    mxc = 0
    for b in range(nblk):
        t0 = b * TB
        n = min(TB, Lp - t0)
        bs = b % NUB
        S.dma('sp', uT[bs][:, :, 0:n + 1], uT_scr[:, :, t0:t0 + n + 1], writes=['uT%d' % bs])
        if use_var:
            for k in range(KC):
                eng = 'pool'
                S.op(eng, lambda e, k=k: e.tensor_tensor(out=dT[:, k, 0:n], in0=uT[bs][:, k, 0:n], in1=uT[bs][:, k, 1:n + 1],
                                                         op=ALU.subtract),
                     reads=['uT%d' % bs], writes=[('dT', k)])
        for v in vars_used:
            if v >= 0:
                ms = mxc % 2
                mxc += 1
                for k in range(KC):
                    eng = 'dve'
                    S.op(eng, lambda e, k=k, v=v, ms=ms: e.scalar_tensor_tensor(
                        out=mx[ms][:, k, 0:n], in0=dT[:, k, 0:n], scalar=muT[:, v, k:k + 1], in1=uT[bs][:, k, 1:n + 1],
                        op0=ALU.mult, op1=ALU.add),
                        reads=[('dT', k), 'muT', 'uT%d' % bs], writes=[('mx', ms, k)])
                src = lambda k, a, bb, ms=ms: mx[ms][:, k, a:bb]
                skeys = [('mx', ms, k) for k in range(KC)]
            else:
                src = lambda k, a, bb: uT[bs][:, k, 1 + a:1 + bb]
                skeys = ['uT%d' % bs]
            for (kind, var, col0, z0) in tiles:
                if var != v:
                    continue
                if kind == 'F':
                    pb = oc % 4
                    oc += 1
                    for k in range(KC):
                        S.op('pe', lambda e, k=k: e.matmul(pz[pb][:, 0:n], lhsT=Wb[:, k, col0:col0 + 128], rhs=src(k, 0, n),
                                                           start=(k == 0), stop=(k == KC - 1)),
                             reads=skeys + Wkeys, writes=['p_z%d' % pb])
                    if pb % 2 == 0:
                        S.op('act', lambda e: e.copy(out=st[pb][:, 0:n], in_=pz[pb][:, 0:n]), reads=['p_z%d' % pb], writes=['st%d' % pb])
                    else:
                        S.op('dve', lambda e: e.tensor_copy(out=st[pb][:, 0:n], in_=pz[pb][:, 0:n]), reads=['p_z%d' % pb], writes=['st%d' % pb])
                    S.dma('act' if pb % 2 == 0 else 'pool', zF[z0:z0 + 128, t0:t0 + n], st[pb][:, 0:n], reads=['st%d' % pb], writes=['z'])
                else:
                    for jt in range(n // 128):
                        pb = oc % 4
                        oc += 1
                        for k in range(KC):
                            S.op('pe', lambda e, k=k: e.matmul(pz[pb][:, :], lhsT=src(k, jt * 128, (jt + 1) * 128),
                                                               rhs=Wb[:, k, col0:col0 + 512],
                                                               start=(k == 0), stop=(k == KC - 1)),
                                 reads=skeys + Wkeys, writes=['p_z%d' % pb])
                        if pb % 2 == 0:
                            S.op('act', lambda e: e.copy(out=st[pb][:], in_=pz[pb][:]), reads=['p_z%d' % pb], writes=['st%d' % pb])
                        else:
                            S.op('dve', lambda e: e.tensor_copy(out=st[pb][:], in_=pz[pb][:]), reads=['p_z%d' % pb], writes=['st%d' % pb])
                        S.dma('act' if pb % 2 == 0 else 'pool', zT[t0 + jt * 128:t0 + (jt + 1) * 128, z0:z0 + 512], st[pb][:], reads=['st%d' % pb], writes=['z'])
    barrier(S)
    A.close()


def emit_hgrn2(S, nc, *, Lp, zF, zT, lbl, n_la, layer_j, gn, og, ident, masks, TB=512, NH=4):
    HW = NH * 128
    A = Alloc(nc)
    f32 = lambda name, shape: A.sb(name, shape, F32)
    bf = lambda name, shape: A.sb(name, shape, BF16)
    W = NH * TB
    qT = f32("qT", [128, NH, TB]); fT = f32("fT", [128, NH, TB]); kk = f32("kk", [128, NH, TB])
    bb = f32("bb", [128, NH, TB]); dd = f32("dd", [128, NH, TB]); EE = f32("EE", [128, NH, TB])
    smask = f32("smask", [128, NH, TB])
    prod = {nm: bf("p_" + nm, [128, NH, TB]) for nm in ('qd', 'kd', 'qo', 'ko', 'qh', 'kh')}
    NCH = TB // 64
    tm = f32("tm", [64, NCH, 2 * HW])
    vb = bf("vb", [64, NCH, HW])
    g2 = f32("g2", [64, NCH, HW])
    gnb = f32("gnb", [64, HW])
    ebl = f32("ebl", [128, NH, NCH])
    lb_t = f32("lb_t", [128, n_la, NH]); lb = f32("lb", [128, NH]); oml = f32("oml", [128, NH])
    lsum = f32("lsum", [128, NH]); lcum = f32("lcum", [128, NH])
    idf = f32("idf", [128, 128]); idb = bf("idb", [128, 128])
    md = f32("md", [64, 64]); mo = f32("mo", [64, 64])
    St = [f32("St%d" % h, [128, 128]) for h in range(NH)]
    Sb = [bf("Sb%d" % h, [128, 128]) for h in range(NH)]
    ktok = [bf("ktok%d" % i, [64, 128]) for i in range(NH)]
    scA = [f32("scA%d" % i, [64, 64]) for i in range(NH)]
    scB = [f32("scB%d" % i, [64, 64]) for i in range(NH)]
    scT = [bf("scT%d" % i, [64, 64]) for i in range(NH)]
    sq = f32("sq", [64, NH, 128])
    ss = [f32("ssq%d" % i, [64, NH]) for i in range(2)]
    ogt = [f32("ogt%d" % i, [64, HW]) for i in range(2)]
    p_s = [A.ps("p_s%d" % i, [64, 2, 64], F32) for i in range(2)]
    p_sd = [p_s[i][:, 0, :] for i in range(2)]
    p_so = [p_s[i][:, 1, :] for i in range(2)]
    p_kt = [A.ps("p_kt%d" % i, [64, 128], BF16) for i in range(2)]
    p_o = [A.ps("p_o%d" % i, [64, NH, 128], F32) for i in range(2)]
    p_ds = [A.ps("p_ds%d" % i, [128, 128], F32) for i in range(2)]

    S.dma('sp', idf[:], ident[:, :], writes=['idf'])
    S.op('dve', lambda e: e.tensor_copy(out=idb[:], in_=idf[:]), reads=['idf'], writes=['idb'])
    for hh in range(NH):
        S.dma('sp', smask[:, hh, :], masks[0, :, 0:TB], writes=[('smask', hh)])
    S.dma('sp', md[:], masks[1, 0:64, 0:64], writes=['md'])
    S.dma('sp', mo[:], masks[2, 0:64, 0:64], writes=['mo'])
    for hh in range(NH):
        S.dma('sp', gnb[:, hh * 128:(hh + 1) * 128], gn[0:1, :].partition_broadcast(64), writes=[('gnb', hh)])
    gnkeys = [('gnb', hh) for hh in range(NH)]
    smkeys = [('smask', hh) for hh in range(NH)]
    S.dma('sp', lb_t[:], lbl[:, :, :], writes=['lb_t'])
    S.op('act', lambda e: e.activation(out=lb_t[:], in_=lb_t[:], func=AF.Exp), reads=['lb_t'], writes=['lb_t'])
    S.op('dve', lambda e: e.tensor_copy(out=lsum[:], in_=lb_t[:, 0, :]), reads=['lb_t'], writes=['lsum'])
    for j in range(1, n_la):
        S.op('dve', lambda e, j=j: e.tensor_tensor(out=lsum[:], in0=lsum[:], in1=lb_t[:, j, :], op=ALU.add),
             reads=['lb_t', 'lsum'], writes=['lsum'])
    S.op('dve', lambda e: e.reciprocal(out=lsum[:], in_=lsum[:]), reads=['lsum'], writes=['lsum'])
    S.op('dve', lambda e: e.memset(lcum[:], 0.0), writes=['lcum'])
    for j in range(0, layer_j + 1):
        S.op('dve', lambda e, j=j: e.tensor_tensor(out=lcum[:], in0=lcum[:], in1=lb_t[:, j, :], op=ALU.add),
             reads=['lb_t', 'lcum'], writes=['lcum'])
    S.op('dve', lambda e: e.tensor_tensor(out=lcum[:], in0=lcum[:], in1=lb_t[:, 0, :], op=ALU.subtract),
         reads=['lb_t', 'lcum'], writes=['lcum'])
    S.op('dve', lambda e: e.tensor_tensor(out=lb[:], in0=lcum[:], in1=lsum[:], op=ALU.mult),
         reads=['lsum', 'lcum'], writes=['lb'])
    S.op('dve', lambda e: e.tensor_scalar(out=oml[:], in0=lb[:], scalar1=-1.0, scalar2=1.0, op0=ALU.mult, op1=ALU.add),
         reads=['lb'], writes=['oml'])
    for h in range(NH):
        S.op('dve', lambda e, h=h: e.memset(St[h][:], 0.0), writes=['St%d' % h])
        S.op('pool', lambda e, h=h: e.memset(Sb[h][:], 0.0), writes=['Sb%d' % h])

    nblk = (Lp + TB - 1) // TB
    cc = 0
    for b in range(nblk):
        t0 = b * TB
        n = min(TB, Lp - t0)
        nch = n // 64
        S.dma('sp', qT[:, :, 0:n], zF[0:HW, t0:t0 + n].rearrange("(h p) t -> p h t", p=128), writes=['qT'])
        S.dma('sp', fT[:, :, 0:n], zF[HW:2 * HW, t0:t0 + n].rearrange("(h p) t -> p h t", p=128), writes=['fT'])
        S.dma('sp', tm[:, 0:nch, :], zT[t0:t0 + n, 0:2 * HW].rearrange("(c p) n -> p c n", p=64), writes=['tm'])
        S.op('act', lambda e: e.activation(out=fT[:, :, 0:n], in_=fT[:, :, 0:n], func=AF.Sigmoid), reads=['fT'], writes=['fT'])
        for h in range(NH):
            S.op('dve', lambda e, h=h: e.tensor_scalar(out=fT[:, h, 0:n], in0=fT[:, h, 0:n], scalar1=oml[:, h:h + 1],
                                                      scalar2=lb[:, h:h + 1], op0=ALU.mult, op1=ALU.add),
                 reads=['fT', 'oml', 'lb'], writes=['fT'])
        S.op('dve', lambda e: e.tensor_scalar(out=kk[:, :, 0:n], in0=fT[:, :, 0:n], scalar1=-1.0, scalar2=1.0,
                                              op0=ALU.mult, op1=ALU.add), reads=['fT'], writes=['kk'])
        S.op('act', lambda e: e.activation(out=fT[:, :, 0:n], in_=fT[:, :, 0:n], func=AF.Ln), reads=['fT'], writes=['fT'])
        for h in range(NH):
            S.op('dve', lambda e, h=h: e.tensor_tensor_scan(out=bb[:, h, 0:n], data0=smask[:, h, 0:n], data1=fT[:, h, 0:n],
                                                           initial=0.0, op0=ALU.mult, op1=ALU.add),
                 reads=['fT'] + smkeys, writes=['bb'])

        def factor(name, src, ref_w, ref_i, sign):
            if ref_w is None:
                S.op('act', lambda e: e.activation(out=EE[:, :, 0:n], in_=bb[:, :, 0:n], func=AF.Exp, scale=float(sign)),
                     reads=['bb'], writes=['EE'])
            else:
                for h in range(NH):
                    b3 = bb[:, h, 0:n].rearrange("p (c w) -> p c w", w=ref_w)
                    d3 = dd[:, h, 0:n].rearrange("p (c w) -> p c w", w=ref_w)
                    S.op('dve', lambda e, b3=b3, d3=d3: e.tensor_tensor(
                        out=d3, in0=b3, in1=b3[:, :, ref_i:ref_i + 1].to_broadcast([128, n // ref_w, ref_w]), op=ALU.subtract),
                        reads=['bb'], writes=['dd'])
                S.op('act', lambda e: e.activation(out=EE[:, :, 0:n], in_=dd[:, :, 0:n], func=AF.Exp, scale=float(sign)),
                     reads=['dd'], writes=['EE'])
            S.op('dve', lambda e: e.tensor_tensor(out=prod[name][:, :, 0:n], in0=src[:, :, 0:n], in1=EE[:, :, 0:n], op=ALU.mult),
                 reads=['EE', 'qT', 'kk'], writes=['pr_' + name])
        factor('qd', qT, 32, 15, 1.0)
        factor('kd', kk, 32, 15, -1.0)
        factor('qo', qT, 64, 31, 1.0)
        factor('ko', kk, 64, 31, -1.0)
        factor('qh', qT, None, None, 1.0)
        factor('kh', kk, 64, 63, -1.0)
        for h in range(NH):
            S.op('act', lambda e, h=h: e.activation(out=ebl[:, h, 0:nch],
                                                   in_=bb[:, h, 0:n].rearrange("p (c w) -> p c w", w=64)[:, :, 63],
                                                   func=AF.Exp), reads=['bb'], writes=['ebl'])
        S.op('act', lambda e: e.copy(out=vb[:, 0:nch, :], in_=tm[:, 0:nch, 0:HW]), reads=['tm'], writes=['vb'])
        S.op('act', lambda e: e.activation(out=g2[:, 0:nch, :], in_=tm[:, 0:nch, HW:2 * HW], func=AF.Silu), reads=['tm'], writes=['g2'])
        S.op('dve', lambda e: e.tensor_tensor(out=g2[:, 0:nch, :], in0=g2[:, 0:nch, :],
                                              in1=gnb[:].unsqueeze(1).to_broadcast([64, nch, HW]), op=ALU.mult),
             reads=['g2'] + gnkeys, writes=['g2'])
        for c in range(nch):
            cs = slice(c * 64, (c + 1) * 64)
            po = cc % 2
            def head_gen(h):
                i2 = (cc * NH + h) % 2
                S.op('pe', lambda e: e.matmul(p_sd[i2], lhsT=prod['kd'][:, h, cs], rhs=prod['qd'][:, h, cs], start=True, stop=True),
                     reads=['pr_kd', 'pr_qd'], writes=['p_s%d' % i2])
                S.op('pe', lambda e: e.matmul(p_so[i2], lhsT=prod['ko'][:, h, cs], rhs=prod['qo'][:, h, cs], start=True, stop=True),
                     reads=['pr_ko', 'pr_qo'], writes=['p_s%d' % i2])
                S.op('pe', lambda e: e.transpose(out=p_kt[i2][:], in_=prod['kh'][:, h, cs], identity=idb[:]),
                     reads=['pr_kh', 'idb'], writes=['p_kt%d' % i2])
                S.op('dve', lambda e: e.tensor_tensor(out=scA[h][:], in0=p_sd[i2], in1=md[:], op=ALU.mult),
                     reads=['p_s%d' % i2, 'md'], writes=['scA%d' % h])
                S.op('dve', lambda e: e.tensor_tensor(out=scB[h][:], in0=p_so[i2], in1=mo[:], op=ALU.mult),
                     reads=['p_s%d' % i2, 'mo'], writes=['scB%d' % h])
                S.op('dve', lambda e: e.tensor_tensor(out=scT[h][:], in0=scA[h][:], in1=scB[h][:], op=ALU.add),
                     reads=['scA%d' % h, 'scB%d' % h], writes=['scT%d' % h])
                S.op('act', lambda e: e.copy(out=ktok[h][:], in_=p_kt[i2][:]), reads=['p_kt%d' % i2], writes=['ktok%d' % h])
                yield
                S.op('pe', lambda e: e.matmul(p_o[po][:, h, :], lhsT=scT[h][:], rhs=vb[:, c, h * 128:(h + 1) * 128], start=True, stop=False),
                     reads=['scT%d' % h, 'vb'], writes=[('p_o', po)])
                S.op('pe', lambda e: e.matmul(p_o[po][:, h, :], lhsT=prod['qh'][:, h, cs], rhs=Sb[h][:], start=False, stop=True),
                     reads=['pr_qh', 'Sb%d' % h], writes=[('p_o', po)])
                S.op('pe', lambda e: e.matmul(p_ds[i2][:], lhsT=ktok[h][:], rhs=vb[:, c, h * 128:(h + 1) * 128], start=True, stop=True),
                     reads=['ktok%d' % h, 'vb'], writes=['p_ds%d' % i2])
                S.op('dve', lambda e: e.scalar_tensor_tensor(out=St[h][:], in0=St[h][:], scalar=ebl[:, h, c:c + 1], in1=p_ds[i2][:],
                                                             op0=ALU.mult, op1=ALU.add),
                     reads=['St%d' % h, 'ebl', 'p_ds%d' % i2], writes=['St%d' % h])
                S.op('act', lambda e: e.copy(out=Sb[h][:], in_=St[h][:]), reads=['St%d' % h], writes=['Sb%d' % h])
            gens = [head_gen(h) for h in range(NH)]
            alive = list(gens)
            while alive:
                nxt_alive = []
                for g_ in alive:
                    try:
                        next(g_)
                        nxt_alive.append(g_)
                    except StopIteration:
                        pass
                alive = nxt_alive
            okeys = [('p_o', po)]
            S.op('act', lambda e: e.activation(out=sq[:], in_=p_o[po][:], func=AF.Square), reads=okeys, writes=['sq'])
            S.op('dve', lambda e: e.tensor_reduce(out=ss[po][:], in_=sq[:], axis=AX.X, op=ALU.add), reads=['sq'], writes=['ss%d' % po])
            S.op('dve', lambda e: e.tensor_scalar(out=ss[po][:], in0=ss[po][:], scalar1=1.0 / 128, scalar2=1e-6, op0=ALU.mult, op1=ALU.add),
                 reads=['ss%d' % po], writes=['ss%d' % po])
            S.op('act', lambda e: e.sqrt(out=ss[po][:], in_=ss[po][:]), reads=['ss%d' % po], writes=['ss%d' % po])
            S.op('dve', lambda e: e.reciprocal(out=ss[po][:], in_=ss[po][:]), reads=['ss%d' % po], writes=['ss%d' % po])
            for h in range(NH):
                S.op('dve', lambda e, h=h: e.scalar_tensor_tensor(out=ogt[po][:, h * 128:(h + 1) * 128], in0=p_o[po][:, h, :],
                                                                 scalar=ss[po][:, h:h + 1], in1=g2[:, c, h * 128:(h + 1) * 128],
                                                                 op0=ALU.mult, op1=ALU.mult),
                     reads=[('p_o', po), 'ss%d' % po, 'g2'], writes=[('ogt', po, h)])
            S.dma('sp', og[t0 + c * 64:t0 + (c + 1) * 64, :], ogt[po][:], reads=[('ogt', po, h) for h in range(NH)], writes=['og'])
            cc += 1
    barrier(S)
    A.close()


def emit_postnorm(S, nc, *, NT, D, y, h, g, out, eps=1e-6):
    A = Alloc(nc)
    yb = [A.sb("yb%d" % i, [128, D], F32) for i in range(2)]
    hb = [A.sb("hb%d" % i, [128, D], F32) for i in range(2)]
    gb = A.sb("gb", [128, D], F32)
    junk = A.sb("junk", [128, D], BF16)
    ss = [A.sb("ss%d" % i, [128, 1], F32) for i in range(2)]
    S.dma('sp', gb[:], g[0:1, :].partition_broadcast(128), writes=['gb'])
    for j in range(NT):
        sl = j % 2
        S.dma('sp', yb[sl][:], y[j * 128:(j + 1) * 128, :], writes=['yb%d' % sl])
        S.dma('sp', hb[sl][:], h[j * 128:(j + 1) * 128, :], writes=['hb%d' % sl])
        S.op('dve', lambda e: e.memset(ss[sl][:], 0.0), writes=['ss%d' % sl])
        S.op('act', lambda e: e.activation(out=junk[:], in_=yb[sl][:], func=AF.Square, accum_out=ss[sl][:]),
             reads=['yb%d' % sl], writes=['junk', 'ss%d' % sl])
        S.op('dve', lambda e: e.tensor_scalar(out=ss[sl][:], in0=ss[sl][:], scalar1=1.0 / D, scalar2=eps,
                                              op0=ALU.mult, op1=ALU.add), reads=['ss%d' % sl], writes=['ss%d' % sl])
        S.op('act', lambda e: e.sqrt(out=ss[sl][:], in_=ss[sl][:]), reads=['ss%d' % sl], writes=['ss%d' % sl])
        S.op('dve', lambda e: e.reciprocal(out=ss[sl][:], in_=ss[sl][:]), reads=['ss%d' % sl], writes=['ss%d' % sl])
        S.op('dve', lambda e: e.scalar_tensor_tensor(out=yb[sl][:], in0=yb[sl][:], scalar=ss[sl][:, 0:1], in1=gb[:],
                                                     op0=ALU.mult, op1=ALU.mult),
             reads=['yb%d' % sl, 'ss%d' % sl, 'gb'], writes=['yb%d' % sl])
        S.op('pool', lambda e: e.tensor_tensor(out=hb[sl][:], in0=hb[sl][:], in1=yb[sl][:], op=ALU.add),
             reads=['yb%d' % sl, 'hb%d' % sl], writes=['hb%d' % sl])
        S.dma('pool', out[j * 128:(j + 1) * 128, :], hb[sl][:], reads=['hb%d' % sl], writes=['out'])
    barrier(S)
    A.close()


def emit_attn(S, nc, *, Lp, zF, zT, lamv, lambda_init, sublng, og, rope, perm, amask, padbias, NH=2, npad=112):
    NT = Lp // 128
    TB = 512
    scale = 128 ** -0.5
    A = Alloc(nc)
    f32 = lambda name, shape: A.sb(name, shape, F32)
    bf = lambda name, shape: A.sb(name, shape, BF16)
    kb = bf("kb", [128, 2, Lp])
    Vb = bf("Vb", [128, NT, 257])
    qb = [bf("qb%d" % i, [128, 2, TB]) for i in range(2)]
    xT = f32("xT", [128, 2, TB]); cs = f32("cs", [128, 2, TB]); t1 = f32("t1", [128, TB]); t2 = f32("t2", [128, TB])
    vst = f32("vst", [128, 8, 256])
    pm = f32("pm", [128, 128])
    mk_f = f32("mk_f", [128, 4, 512]); mk = bf("mk", [128, 4, 512])
    pbias = f32("pbias", [128, 1]); zbias = f32("zbias", [128, 1])
    PT = [bf("PT%d" % i, [128, 512]) for i in range(3)]
    oc = [f32("oc%d" % i, [128, 4, 256]) for i in range(2)]
    den = f32("den", [128, 8])
    gt = f32("gt", [128, 4, 256]); g2 = f32("g2", [128, 4, 256]); sgb = f32("sgb", [128, 256])
    av = f32("av", [128, 4, 256]); sq = f32("sq", [128, 4, 256]); ssq = f32("ssq", [128, 4])
    ogt = f32("ogt", [128, 4, 256])
    lv = f32("lv", [128, 4, 128]); lp_ = f32("lp_", [128, 2, 128]); ld = f32("ld", [128, 2]); nlam = f32("nlam", [128, 1])
    p_s = [A.ps("p_s%d" % i, [128, 512], F32) for i in range(2)]
    p_a = [A.ps("p_a%d" % i, [128, 512], F32) for i in range(4)]
    p_r = A.ps("p_r", [128, 512], F32)

    S.dma('sp', pm[:], perm[:, :], writes=['pm'])
    S.dma('sp', mk_f[:], amask.rearrange("i p q -> p i q"), writes=['mk_f'])
    S.op('dve', lambda e: e.tensor_copy(out=mk[:], in_=mk_f[:]), reads=['mk_f'], writes=['mk'])
    S.dma('sp', pbias[:], padbias[:, :], writes=['pbias'])
    S.op('pool', lambda e: e.memset(zbias[:], 0.0), writes=['zbias'])
    S.dma('sp', sgb[:], sublng[0:1, :].partition_broadcast(128), writes=['sgb'])
    S.op('dve', lambda e: e.tensor_scalar(out=sgb[:], in0=sgb[:], scalar1=float(1.0 - lambda_init), scalar2=None, op0=ALU.mult),
         reads=['sgb'], writes=['sgb'])
    for i in range(4):
        S.dma('sp', lv[:, i, :], lamv[i:i + 1, :].partition_broadcast(128), writes=[('lv', i)])
    S.op('dve', lambda e: e.tensor_tensor(out=lp_[:, 0, :], in0=lv[:, 0, :], in1=lv[:, 1, :], op=ALU.mult),
         reads=[('lv', 0), ('lv', 1)], writes=['lp0'])
    S.op('dve', lambda e: e.tensor_tensor(out=lp_[:, 1, :], in0=lv[:, 2, :], in1=lv[:, 3, :], op=ALU.mult),
         reads=[('lv', 2), ('lv', 3)], writes=['lp1'])
    S.op('dve', lambda e: e.tensor_reduce(out=ld[:], in_=lp_[:], axis=AX.X, op=ALU.add), reads=['lp0', 'lp1'], writes=['ld'])
    S.op('act', lambda e: e.activation(out=ld[:], in_=ld[:], func=AF.Exp), reads=['ld'], writes=['ld'])
    S.op('dve', lambda e: e.tensor_tensor(out=nlam[:], in0=ld[:, 1:2], in1=ld[:, 0:1], op=ALU.subtract), reads=['ld'], writes=['nlam'])
    S.op('dve', lambda e: e.tensor_scalar(out=nlam[:], in0=nlam[:], scalar1=float(-lambda_init), scalar2=None, op0=ALU.add),
         reads=['nlam'], writes=['nlam'])

    nblk = (Lp + TB - 1) // TB

    def rope_block(row0, t0, n, dst, dkey):
        S.dma('sp', xT[:, :, 0:n], zF[row0:row0 + 256, t0:t0 + n].rearrange("(c p) t -> p c t", p=128), writes=['xT'])
        S.dma('sp', cs[:, :, 0:n], rope[:, :, t0:t0 + n].rearrange("a p t -> p a t"), writes=['cs'])
        for c in range(2):
            S.op('pe', lambda e: e.matmul(p_r[:, 0:n], lhsT=pm[:], rhs=xT[:, c, 0:n], start=True, stop=True),
                 reads=['pm', 'xT'], writes=['p_r'])
            S.op('dve', lambda e: e.tensor_tensor(out=t1[:, 0:n], in0=xT[:, c, 0:n], in1=cs[:, 0, 0:n], op=ALU.mult),
                 reads=['xT', 'cs'], writes=['t1'])
            S.op('dve', lambda e: e.tensor_tensor(out=t2[:, 0:n], in0=p_r[:, 0:n], in1=cs[:, 1, 0:n], op=ALU.mult),
                 reads=['p_r', 'cs'], writes=['t2'])
            S.op('pool', lambda e: e.tensor_tensor(out=dst(c), in0=t1[:, 0:n], in1=t2[:, 0:n], op=ALU.add),
                 reads=['t1', 't2'], writes=[dkey])

    pc = 0
    for hd in range(NH):
        for b in range(nblk):
            t0 = b * TB
            n = min(TB, Lp - t0)
            rope_block(NH * 256 + hd * 256, t0, n, lambda c: kb[:, c, t0:t0 + n], 'kb')
        for j0 in range(0, NT, 8):
            nj = min(8, NT - j0)
            S.dma('sp', vst[:, 0:nj, :], zT[j0 * 128:(j0 + nj) * 128, hd * 256:(hd + 1) * 256].rearrange("(j p) e -> p j e", p=128),
                  writes=['vst'])
            S.op('act', lambda e: e.copy(out=Vb[:, j0:j0 + nj, 0:256], in_=vst[:, 0:nj, :]), reads=['vst'], writes=['Vb'])
        S.op('pool', lambda e: e.memset(Vb[:, :, 256:257], 1.0), writes=['Vb'])
        for b in range(nblk):
            t0 = b * TB
            n = min(TB, Lp - t0)
            nq = n // 128
            qs = b % 2
            rope_block(hd * 256, t0, n, lambda c: qb[qs][:, c, 0:n], 'qb%d' % qs)
            S.dma('sp', gt[:, 0:nq, :], zT[t0:t0 + n, NH * 256 + hd * 256:NH * 256 + (hd + 1) * 256].rearrange("(j p) e -> p j e", p=128),
                  writes=['gt'])
            S.op('act', lambda e: e.activation(out=g2[:, 0:nq, :], in_=gt[:, 0:nq, :], func=AF.Silu), reads=['gt'], writes=['g2'])
            S.op('dve', lambda e: e.tensor_tensor(out=g2[:, 0:nq, :], in0=g2[:, 0:nq, :],
                                                  in1=sgb[:].unsqueeze(1).to_broadcast([128, nq, 256]), op=ALU.mult),
                 reads=['g2', 'sgb'], writes=['g2'])
            nkt = 4 * b + nq
            for c in range(2):
                idx = []
                for kt in range(nkt):
                    idx.append((pc % 2, pc % 3))
                    pc += 1

                def emit_s(kt):
                    ps = idx[kt][0]
                    S.op('pe', lambda e: e.matmul(p_s[ps][:, 0:n], lhsT=kb[:, c, kt * 128:(kt + 1) * 128], rhs=qb[qs][:, c, 0:n],
                                                  start=True, stop=True),
                         reads=['kb', 'qb%d' % qs], writes=['p_s%d' % ps])
                emit_s(0)
                for kt in range(nkt):
                    ps, pt = idx[kt]
                    if kt + 1 < nkt:
                        emit_s(kt + 1)
                    S.op('act', lambda e: e.activation(out=PT[pt][:, 0:n], in_=p_s[ps][:, 0:n], func=AF.Exp, scale=float(scale),
                                                       bias=(pbias[:, 0:1] if kt == 0 else zbias[:, 0:1])),
                         reads=['p_s%d' % ps, 'pbias', 'zbias'], writes=['PT%d' % pt])
                    i = kt - 4 * b
                    if i >= 0:
                        S.op('dve', lambda e: e.tensor_tensor(out=PT[pt][:, 0:n], in0=PT[pt][:, 0:n], in1=mk[:, i, 0:n], op=ALU.mult),
                             reads=['PT%d' % pt, 'mk'], writes=['PT%d' % pt])
                    for j in range(nq):
                        if i > j:
                            continue
                        last = (kt == 4 * b + j)
                        S.op('pe', lambda e: e.matmul(p_a[j][:, 0:257], lhsT=PT[pt][:, j * 128:(j + 1) * 128], rhs=Vb[:, kt, :],
                                                      start=(kt == 0), stop=last),
                             reads=['PT%d' % pt, 'Vb'], writes=['p_a%d' % j])
                for j in range(nq):
                    S.op('dve', lambda e: e.tensor_scalar(out=den[:, c * 4 + j:c * 4 + j + 1], in0=p_a[j][:, 256:257], scalar1=1e-30, scalar2=None,
                                                          op0=ALU.add), reads=['p_a%d' % j], writes=[('den', c, j)])
                    S.op('dve', lambda e: e.reciprocal(out=den[:, c * 4 + j:c * 4 + j + 1], in_=den[:, c * 4 + j:c * 4 + j + 1]),
                         reads=[('den', c, j)], writes=[('den', c, j)])
                    S.op('act', lambda e: e.activation(out=oc[c][:, j, :], in_=p_a[j][:, 0:256], func=AF.Copy,
                                                       scale=den[:, c * 4 + j:c * 4 + j + 1]),
                         reads=['p_a%d' % j, ('den', c, j)], writes=[('oc', c, j)])
            ock = [('oc', c, j) for c in range(2) for j in range(nq)]
            S.op('dve', lambda e: e.scalar_tensor_tensor(out=av[:, 0:nq, :], in0=oc[1][:, 0:nq, :], scalar=nlam[:, 0:1], in1=oc[0][:, 0:nq, :],
                                                         op0=ALU.mult, op1=ALU.add), reads=ock + ['nlam'], writes=['av'])
            S.op('act', lambda e: e.activation(out=sq[:, 0:nq, :], in_=av[:, 0:nq, :], func=AF.Square), reads=['av'], writes=['sq'])
            S.op('dve', lambda e: e.tensor_reduce(out=ssq[:, 0:nq], in_=sq[:, 0:nq, :], axis=AX.X, op=ALU.add), reads=['sq'], writes=['ssq'])
            S.op('dve', lambda e: e.tensor_scalar(out=ssq[:, 0:nq], in0=ssq[:, 0:nq], scalar1=1.0 / 256, scalar2=1e-5, op0=ALU.mult, op1=ALU.add),
                 reads=['ssq'], writes=['ssq'])
            S.op('act', lambda e: e.sqrt(out=ssq[:, 0:nq], in_=ssq[:, 0:nq]), reads=['ssq'], writes=['ssq'])
            S.op('dve', lambda e: e.reciprocal(out=ssq[:, 0:nq], in_=ssq[:, 0:nq]), reads=['ssq'], writes=['ssq'])
            for j in range(nq):
                S.op('dve', lambda e, j=j: e.scalar_tensor_tensor(out=ogt[:, j, :], in0=av[:, j, :], scalar=ssq[:, j:j + 1], in1=g2[:, j, :],
                                                                 op0=ALU.mult, op1=ALU.mult),
                     reads=['av', 'ssq', 'g2'], writes=['ogt'])
            S.dma('sp', og[t0:t0 + n, hd * 256:(hd + 1) * 256].rearrange("(j p) e -> p j e", p=128), ogt[:, 0:nq, :],
                  reads=['ogt'], writes=['og'])
    barrier(S)
    A.close()


def emit_rwkv(S, nc, *, Lp, zF, zT, w2, a2, vecs, lnx, og, cmat, TB=512):
    NP = 4
    NCH = TB // 64
    A = Alloc(nc)
    f32 = lambda name, shape: A.sb(name, shape, F32)
    bf = lambda name, shape: A.sb(name, shape, BF16)
    cm = f32("cm", [128, 6, 128])
    idb = bf("idb", [128, 128]); onesb = bf("onesb", [128, 1])
    w2f = f32("w2f", [128, 512]); a2f = f32("a2f", [128, 512]); w2b = bf("w2b", [128, 512]); a2b = bf("a2b", [128, 512])
    vc = f32("vc", [128, 5, 4]); omka = f32("omka", [128, 4])
    lnr = f32("lnr", [128, 2, 512])
    zw = f32("zw", [128, 2, TB]); zwb = bf("zwb", [128, 2, TB])
    rT = f32("rT", [128, TB]); kT = f32("kT", [128, TB])
    lw = f32("lw", [128, TB]); G = f32("G", [128, TB]); Gx = f32("Gx", [128, TB])
    eG = f32("eG", [128, TB]); eGx = f32("eGx", [128, TB]); enG = f32("enG", [128, TB])
    av = f32("av", [128, TB]); kkv = f32("kkv", [128, TB]); tmp = f32("tmp", [128, TB]); inv = f32("inv", [128, TB])
    kp = f32("kp", [128, TB]); bet = f32("bet", [128, TB])
    smask = f32("smask", [128, TB])
    egc = f32("egc", [128, NP, NCH])
    BD = {nm: bf("bd_" + nm, [128, NP, NCH, 128]) for nm in ('KT', 'RT', 'BT', 'KK', 'RKR')}
    Vst = f32("Vst", [128, NCH, NP, 128]); Gst = f32("Gst", [128, NCH, NP, 128]); Vbd = bf("Vbd", [128, NCH, NP, 128])
    OG = f32("OG", [128, NCH, NP, 128])
    Zf = [f32("Zf%d" % p, [128, 128]) for p in range(NP)]
    Zs = [f32("Zs%d" % p, [128, 128]) for p in range(NP)]
    Zb = [bf("Zb%d" % p, [128, 128]) for p in range(NP)]
    NR = 4
    Xb = [[bf("Xb%d_%d" % (r, i), [128, 128]) for i in range(5)] for r in range(NR)]
    Yb = [[bf("Yb%d_%d" % (r, i), [128, 128]) for i in range(5)] for r in range(NR)]
    IX = [[bf("IX%d_%d" % (r, i), [128, 128]) for i in range(6)] for r in range(NR)]
    AakT = [bf("AakT%d" % r, [128, 128]) for r in range(NR)]
    BrbT = [bf("BrbT%d" % r, [128, 128]) for r in range(NR)]
    BrkT = [bf("BrkT%d" % r, [128, 128]) for r in range(NR)]
    Bb = [[bf("Bb%d_%d" % (r, i), [128, 128]) for i in range(2)] for r in range(NR)]
    BTt = [bf("BTt%d" % r, [128, 128]) for r in range(NR)]
    KKt = [bf("KKt%d" % r, [128, 128]) for r in range(NR)]
    sqb = [f32("sqb%d" % r, [128, 128]) for r in range(4)]; yb = [f32("yb%d" % r, [128, 128]) for r in range(NR)]
    s1 = f32("s1", [128, NP]); s2 = f32("s2", [128, NP]); mean = f32("mean", [128, NP]); rstd = f32("rstd", [128, NP])
    rkc = f32("rkc", [128, NP])
    ps = [A.ps("ps%d" % i, [128, 4, 128], F32) for i in range(5)]
    po = A.ps("po", [128, 4, 128], F32)
    pst = [A.ps("pst%d" % i, [128, 4, 128], BF16) for i in range(1)]
    pbig = A.ps("pbig", [128, 512], F32)
    slot_ctr = [0]

    def slot():
        i = slot_ctr[0] % 20
        slot_ctr[0] += 1
        return ps[i % 5][:, i // 5, :], ('p_sb', i % 5)

    S.dma('sp', cm[:], cmat.rearrange("i p q -> p i q"), writes=['cm'])
    S.op('dve', lambda e: e.tensor_copy(out=idb[:], in_=cm[:, 0, :]), reads=['cm'], writes=['idb'])
    S.op('pool', lambda e: e.memset(onesb[:], 1.0), writes=['onesb'])
    S.dma('sp', w2f[:], w2[:, :], writes=['w2f'])
    S.dma('sp', a2f[:], a2[:, :], writes=['a2f'])
    S.op('dve', lambda e: e.tensor_copy(out=w2b[:], in_=w2f[:]), reads=['w2f'], writes=['w2b'])
    S.op('dve', lambda e: e.tensor_copy(out=a2b[:], in_=a2f[:]), reads=['a2f'], writes=['a2b'])
    S.dma('sp', vc[:], vecs[:, :, :], writes=['vc'])
    S.op('dve', lambda e: e.tensor_scalar(out=omka[:], in0=vc[:, 3, :], scalar1=-1.0, scalar2=1.0, op0=ALU.mult, op1=ALU.add),
         reads=['vc'], writes=['omka'])
    for i in range(2):
        S.dma('sp', lnr[:, i, :], lnx[i:i + 1, :].partition_broadcast(128), writes=[('lnr', i)])
    lnk = [('lnr', 0), ('lnr', 1)]
    S.op('pool', lambda e: e.memset(smask[:], 1.0), writes=['smask'])
    S.op('pool', lambda e: e.memset(smask[:].rearrange("p (c w) -> p c w", w=64)[:, :, 0:1], 0.0), writes=['smask'])
    for nm in BD:
        S.op('pool', lambda e, nm=nm: e.memset(BD[nm][:], 0.0), writes=[('bd', nm, p) for p in range(NP)])
    S.op('pool', lambda e: e.memset(Vst[:], 0.0), writes=[('Vst', p, hf) for p in range(NP) for hf in range(2)])
    S.op('pool', lambda e: e.memset(Gst[:], 0.0), writes=[('Gst', p, hf) for p in range(NP) for hf in range(2)])
    for p in range(NP):
        S.op('dve', lambda e, p=p: e.memset(Zf[p][:], 0.0), writes=['Zf%d' % p])
        S.op('dve', lambda e, p=p: e.memset(Zb[p][:], 0.0), writes=['Zb%d' % p])

    nblk = (Lp + TB - 1) // TB
    rr = 0
    for b in range(nblk):
        t0 = b * TB
        n = min(TB, Lp - t0)
        nch = n // 64
        for (dst, c0, key) in ((Vst, 0, 'Vst'), (Gst, 512, 'Gst')):
            src = zT[t0:t0 + n, c0:c0 + 512].rearrange("(c s) (p two n) -> s c p two n", s=64, two=2, n=64)
            for p in range(NP):
                S.dma('sp', dst[0:64, 0:nch, p, 0:64], src[:, :, p, 0, :], writes=[(key, p, 0)])
                S.dma('sp', dst[64:128, 0:nch, p, 64:128], src[:, :, p, 1, :], writes=[(key, p, 1)])
        vk = [('Vst', p, hf) for p in range(NP) for hf in range(2)]
        gk = [('Gst', p, hf) for p in range(NP) for hf in range(2)]
        S.op('act', lambda e: e.copy(out=Vbd[:, 0:nch], in_=Vst[:, 0:nch]), reads=vk, writes=['Vbd'])
        S.op('act', lambda e: e.activation(out=Gst[:, 0:nch], in_=Gst[:, 0:nch], func=AF.Silu), reads=gk, writes=gk)
        S.dma('sp', zw[:, :, 0:n], zF[1024:1280, t0:t0 + n].rearrange("(a p) t -> p a t", p=128), writes=['zw'])
        S.op('act', lambda e: e.activation(out=zwb[:, 0, 0:n], in_=zw[:, 0, 0:n], func=AF.Tanh), reads=['zw'], writes=['zwb0'])
        S.op('dve', lambda e: e.tensor_copy(out=zwb[:, 1, 0:n], in_=zw[:, 1, 0:n]), reads=['zw'], writes=['zwb1'])
        for p in range(NP):
            S.dma('sp', rT[:, 0:n], zF[p * 128:(p + 1) * 128, t0:t0 + n], writes=['rT'])
            S.dma('sp', kT[:, 0:n], zF[512 + p * 128:512 + (p + 1) * 128, t0:t0 + n], writes=['kT'])
            S.op('pe', lambda e: e.matmul(pbig[:, 0:n], lhsT=w2b[:, p * 128:(p + 1) * 128], rhs=zwb[:, 0, 0:n], start=True, stop=True),
                 reads=['w2b', 'zwb0'], writes=['p_big'])
            S.op('act', lambda e: e.activation(out=lw[:, 0:n], in_=pbig[:, 0:n], func=AF.Sigmoid, bias=vc[:, 0, p:p + 1]),
                 reads=['p_big', 'vc'], writes=['lw'])
            S.op('pe', lambda e: e.matmul(pbig[:, 0:n], lhsT=a2b[:, p * 128:(p + 1) * 128], rhs=zwb[:, 1, 0:n], start=True, stop=True),
                 reads=['a2b', 'zwb1', 'lw'], writes=['p_big'])
            S.op('act', lambda e: e.activation(out=av[:, 0:n], in_=pbig[:, 0:n], func=AF.Sigmoid, bias=vc[:, 1, p:p + 1]),
                 reads=['p_big', 'vc'], writes=['av'])
            S.op('dve', lambda e: e.tensor_scalar(out=lw[:, 0:n], in0=lw[:, 0:n], scalar1=-0.606531, scalar2=None, op0=ALU.mult),
                 reads=['lw'], writes=['lw'])
            S.op('dve', lambda e: e.tensor_tensor_scan(out=G[:, 0:n], data0=smask[:, 0:n], data1=lw[:, 0:n], initial=0.0,
                                                       op0=ALU.mult, op1=ALU.add), reads=['lw', 'smask'], writes=['G'])
            S.op('pool', lambda e: e.tensor_tensor(out=Gx[:, 0:n], in0=G[:, 0:n], in1=lw[:, 0:n], op=ALU.subtract),
                 reads=['G', 'lw'], writes=['Gx'])
            S.op('act', lambda e: e.activation(out=eG[:, 0:n], in_=G[:, 0:n], func=AF.Exp), reads=['G'], writes=['eG'])
            S.op('act', lambda e: e.activation(out=enG[:, 0:n], in_=G[:, 0:n], func=AF.Exp, scale=-1.0), reads=['G'], writes=['enG'])
            S.op('act', lambda e: e.activation(out=eGx[:, 0:n], in_=Gx[:, 0:n], func=AF.Exp), reads=['Gx'], writes=['eGx'])
            S.op('act', lambda e: e.copy(out=egc[:, p, 0:nch], in_=eG[:, 0:n].rearrange("p (c w) -> p c w", w=64)[:, :, 63]),
                 reads=['eG'], writes=['egc'])
            S.op('dve', lambda e: e.tensor_scalar(out=kkv[:, 0:n], in0=kT[:, 0:n], scalar1=vc[:, 2, p:p + 1], scalar2=None, op0=ALU.mult),
                 reads=['kT', 'vc'], writes=['kkv'])
            S.op('pool', lambda e: e.tensor_tensor(out=tmp[:, 0:n], in0=kkv[:, 0:n], in1=kkv[:, 0:n], op=ALU.mult),
                 reads=['kkv'], writes=['tmp'])
            S.op('pe', lambda e: e.matmul(pbig[:, 0:n], lhsT=cm[:, 1, :], rhs=tmp[:, 0:n], start=True, stop=True),
                 reads=['cm', 'tmp', 'av'], writes=['p_big'])
            S.op('act', lambda e: e.sqrt(out=inv[:, 0:n], in_=pbig[:, 0:n]), reads=['p_big'], writes=['inv'])
            S.op('dve', lambda e: e.tensor_scalar(out=inv[:, 0:n], in0=inv[:, 0:n], scalar1=1e-12, scalar2=None, op0=ALU.max),
                 reads=['inv'], writes=['inv'])
            S.op('dve', lambda e: e.reciprocal(out=inv[:, 0:n], in_=inv[:, 0:n]), reads=['inv'], writes=['inv'])
            S.op('dve', lambda e: e.tensor_tensor(out=kkv[:, 0:n], in0=kkv[:, 0:n], in1=inv[:, 0:n], op=ALU.mult),
                 reads=['kkv', 'inv'], writes=['kkv'])
            S.op('dve', lambda e: e.tensor_scalar(out=kp[:, 0:n], in0=av[:, 0:n], scalar1=vc[:, 3, p:p + 1], scalar2=omka[:, p:p + 1],
                                                  op0=ALU.mult, op1=ALU.add), reads=['av', 'vc', 'omka'], writes=['kp'])
            S.op('pool', lambda e: e.tensor_tensor(out=kp[:, 0:n], in0=kp[:, 0:n], in1=kT[:, 0:n], op=ALU.mult),
                 reads=['kp', 'kT'], writes=['kp'])
            S.op('dve', lambda e: e.scalar_tensor_tensor(out=bet[:, 0:n], in0=kkv[:, 0:n], scalar=-1.0, in1=av[:, 0:n],
                                                         op0=ALU.mult, op1=ALU.mult), reads=['kkv', 'av'], writes=['bet'])
            S.op('dve', lambda e: e.scalar_tensor_tensor(out=tmp[:, 0:n], in0=rT[:, 0:n], scalar=vc[:, 4, p:p + 1], in1=kp[:, 0:n],
                                                         op0=ALU.mult, op1=ALU.mult), reads=['rT', 'vc', 'kp', 'tmp'], writes=['tmp'])

            def bdw(nm, a_, b_, eng):
                for hf in range(2):
                    pr = slice(hf * 64, (hf + 1) * 64)
                    dst = BD[nm][pr, p, 0:nch, hf * 64:(hf + 1) * 64]
                    va = a_[pr, 0:n].rearrange("p (c w) -> p c w", w=64)
                    if b_ is None:
                        S.op(eng, lambda e: e.tensor_copy(out=dst, in_=va), reads=['tmp'], writes=[('bd', nm, p)])
                    else:
                        vb_ = b_[pr, 0:n].rearrange("p (c w) -> p c w", w=64)
                        S.op(eng, lambda e: e.tensor_tensor(out=dst, in0=va, in1=vb_, op=ALU.mult),
                             reads=['kkv', 'eGx', 'rT', 'eG', 'bet', 'enG', 'kp'], writes=[('bd', nm, p)])
            bdw('KT', kkv, eGx, 'dve')
            bdw('RT', rT, eG, 'pool')
            bdw('BT', bet, enG, 'dve')
            bdw('KK', kp, enG, 'pool')
            bdw('RKR', tmp, None, 'dve')
        for c in range(nch):
            okeys = [None] * NP
            oslots = [None] * NP
            rk_ps, rk_key = pbig, 'p_big'
            def pair_gen(p):
                r = p
                KT = BD['KT'][:, p, c, :]; RT = BD['RT'][:, p, c, :]; BT = BD['BT'][:, p, c, :]; KK = BD['KK'][:, p, c, :]
                bk = lambda nm: [('bd', nm, p)]
                S.op('pe', lambda e: e.matmul(rk_ps[:, p:p + 1], lhsT=BD['RKR'][:, p, c, :], rhs=onesb[:], start=True, stop=True),
                     reads=bk('RKR') + ['onesb'], writes=[rk_key])
                x0, x0k = slot(); y0, y0k = slot(); ak, akk = slot(); rb, rbk = slot(); rkk_, rkkk = slot()
                S.op('pe', lambda e: e.matmul(x0, lhsT=BT, rhs=KT, start=True, stop=True), reads=bk('BT') + bk('KT'), writes=[x0k])
                S.op('pe', lambda e: e.matmul(y0, lhsT=KT, rhs=BT, start=True, stop=True), reads=bk('BT') + bk('KT'), writes=[y0k])
                S.op('pe', lambda e: e.matmul(ak, lhsT=KK, rhs=KT, start=True, stop=True), reads=bk('KK') + bk('KT'), writes=[akk])
                S.op('pe', lambda e: e.matmul(rb, lhsT=BT, rhs=RT, start=True, stop=True), reads=bk('BT') + bk('RT'), writes=[rbk])
                S.op('pe', lambda e: e.matmul(rkk_, lhsT=KK, rhs=RT, start=True, stop=True), reads=bk('KK') + bk('RT'), writes=[rkkk])
                S.op('dve', lambda e: e.tensor_tensor(out=Xb[r][0][:], in0=x0, in1=cm[:, 2, :], op=ALU.mult), reads=[x0k, 'cm'], writes=[('X', r, 0)])
                S.op('dve', lambda e: e.tensor_tensor(out=Yb[r][0][:], in0=y0, in1=cm[:, 4, :], op=ALU.mult), reads=[y0k, 'cm'], writes=[('Y', r, 0)])
                S.op('dve', lambda e: e.tensor_tensor(out=AakT[r][:], in0=ak, in1=cm[:, 2, :], op=ALU.mult), reads=[akk, 'cm'], writes=[('Aak', r)])
                S.op('dve', lambda e: e.tensor_tensor(out=BrbT[r][:], in0=rb, in1=cm[:, 3, :], op=ALU.mult), reads=[rbk, 'cm'], writes=[('Brb', r)])
                S.op('dve', lambda e: e.tensor_tensor(out=BrkT[r][:], in0=rkk_, in1=cm[:, 3, :], op=ALU.mult), reads=[rkkk, 'cm'], writes=[('Brk', r)])
                S.op('pool', lambda e: e.tensor_tensor(out=IX[r][0][:], in0=Xb[r][0][:], in1=idb[:], op=ALU.add),
                     reads=[('X', r, 0), 'idb'], writes=[('IX', r, 0)])
                yield
                b0, b0k = slot()
                S.op('pe', lambda e: e.matmul(b0, lhsT=KT, rhs=Zb[p][:], start=True, stop=False), reads=bk('KT') + ['Zb%d' % p], writes=[b0k])
                S.op('pe', lambda e: e.matmul(b0, lhsT=AakT[r][:], rhs=Vbd[:, c, p, :], start=False, stop=True),
                     reads=[('Aak', r), 'Vbd'], writes=[b0k])
                S.op('act', lambda e: e.copy(out=Bb[r][0][:], in_=b0), reads=[b0k], writes=[('B', r, 0)])
                yield
                for i in range(5):
                    xs, xsk = slot()
                    S.op('pe', lambda e: e.matmul(xs, lhsT=Yb[r][i][:], rhs=Xb[r][i][:], start=True, stop=True),
                         reads=[('X', r, i), ('Y', r, i)], writes=[xsk])
                    if i < 4:
                        ys, ysk = slot()
                        S.op('pe', lambda e: e.matmul(ys, lhsT=Xb[r][i][:], rhs=Yb[r][i][:], start=True, stop=True),
                             reads=[('X', r, i), ('Y', r, i)], writes=[ysk])
                        S.op('act', lambda e: e.copy(out=Xb[r][i + 1][:], in_=xs), reads=[xsk], writes=[('X', r, i + 1)])
                        S.op('act', lambda e: e.copy(out=Yb[r][i + 1][:], in_=ys), reads=[ysk], writes=[('Y', r, i + 1)])
                        S.op('pool', lambda e: e.tensor_tensor(out=IX[r][i + 1][:], in0=Xb[r][i + 1][:], in1=idb[:], op=ALU.add),
                             reads=[('X', r, i + 1), 'idb'], writes=[('IX', r, i + 1)])
                    else:
                        S.op('dve', lambda e: e.tensor_tensor(out=IX[r][5][:], in0=xs, in1=idb[:], op=ALU.add),
                             reads=[xsk, 'idb'], writes=[('IX', r, 5)])
                    yield
                cur = 0
                for i in range(6):
                    bs, bsk = slot()
                    S.op('pe', lambda e: e.matmul(bs, lhsT=IX[r][i][:], rhs=Bb[r][cur][:], start=True, stop=True),
                         reads=[('IX', r, i), ('B', r, cur)], writes=[bsk])
                    nxt = 1 - cur
                    if i % 2 == 0:
                        S.op('act', lambda e: e.copy(out=Bb[r][nxt][:], in_=bs), reads=[bsk], writes=[('B', r, nxt)])
                    else:
                        S.op('dve', lambda e: e.tensor_copy(out=Bb[r][nxt][:], in_=bs), reads=[bsk], writes=[('B', r, nxt)])
                    cur = nxt
                    yield
                U = Bb[r][cur]
                Uk = ('B', r, cur)
                o_ps, o_k = po[:, p, :], 'p_o'
                S.op('pe', lambda e: e.matmul(o_ps, lhsT=RT, rhs=Zb[p][:], start=True, stop=False), reads=bk('RT') + ['Zb%d' % p], writes=[o_k])
                S.op('pe', lambda e: e.matmul(o_ps, lhsT=BrbT[r][:], rhs=U[:], start=False, stop=False), reads=[('Brb', r), Uk], writes=[o_k])
                S.op('pe', lambda e: e.matmul(o_ps, lhsT=BrkT[r][:], rhs=Vbd[:, c, p, :], start=False, stop=True), reads=[('Brk', r), 'Vbd'], writes=[o_k])
                okeys[p] = o_k
                oslots[p] = o_ps
                S.op('dve', lambda e: e.tensor_reduce(out=s1[:, p:p + 1], in_=o_ps, axis=AX.X, op=ALU.add), reads=[o_k], writes=[('s1', p)])
                S.op('act', lambda e: e.activation(out=sqb[p][:], in_=o_ps, func=AF.Square), reads=[o_k], writes=[('sqb', p)])
                S.op('dve', lambda e: e.tensor_reduce(out=s2[:, p:p + 1], in_=sqb[p][:], axis=AX.X, op=ALU.add), reads=[('sqb', p)], writes=[('s2', p)])
                yield
                S.op('pe', lambda e: e.transpose(out=pst[0][:, 0, :], in_=BT, identity=idb[:]), reads=bk('BT') + ['idb'], writes=['p_st'])
                S.op('pe', lambda e: e.transpose(out=pst[0][:, 1, :], in_=KK, identity=idb[:]), reads=bk('KK') + ['idb'], writes=['p_st'])
                S.op('act', lambda e: e.copy(out=BTt[r][:], in_=pst[0][:, 0, :]), reads=['p_st'], writes=[('BTt', r)])
                S.op('dve', lambda e: e.tensor_copy(out=KKt[r][:], in_=pst[0][:, 1, :]), reads=['p_st'], writes=[('KKt', r)])
                yield
                d_ps, d_k = slot()
                S.op('pe', lambda e: e.matmul(d_ps, lhsT=BTt[r][:], rhs=U[:], start=True, stop=False), reads=[('BTt', r), Uk], writes=[d_k])
                S.op('pe', lambda e: e.matmul(d_ps, lhsT=KKt[r][:], rhs=Vbd[:, c, p, :], start=False, stop=True), reads=[('KKt', r), 'Vbd'], writes=[d_k])
                S.op('act', lambda e: e.activation(out=Zs[p][:], in_=Zf[p][:], func=AF.Copy, scale=egc[:, p, c:c + 1]),
                     reads=['Zf%d' % p, 'egc'], writes=['Zs%d' % p])
                S.op('dve', lambda e: e.scalar_tensor_tensor(out=Zf[p][:], in0=d_ps, scalar=egc[:, p, c:c + 1], in1=Zs[p][:],
                                                             op0=ALU.mult, op1=ALU.add), reads=[d_k, 'egc', 'Zs%d' % p], writes=['Zf%d' % p])
                S.op('act', lambda e: e.copy(out=Zb[p][:], in_=Zf[p][:]), reads=['Zf%d' % p], writes=['Zb%d' % p])
            gens = [pair_gen(p) for p in range(NP)]
            alive = list(gens)
            while alive:
                nxt_alive = []
                for g_ in alive:
                    try:
                        next(g_)
                        nxt_alive.append(g_)
                    except StopIteration:
                        pass
                alive = nxt_alive
            sk1 = [('s1', p) for p in range(NP)]; sk2 = [('s2', p) for p in range(NP)]
            S.op('act', lambda e: e.copy(out=rkc[:], in_=rk_ps[:, 0:NP]), reads=[rk_key], writes=['rkc'])
            S.op('dve', lambda e: e.tensor_scalar(out=mean[:], in0=s1[:], scalar1=1.0 / 64, scalar2=None, op0=ALU.mult), reads=sk1, writes=['mean'])
            S.op('dve', lambda e: e.tensor_tensor(out=rstd[:], in0=mean[:], in1=mean[:], op=ALU.mult), reads=['mean'], writes=['rstd'])
            S.op('dve', lambda e: e.scalar_tensor_tensor(out=rstd[:], in0=s2[:], scalar=1.0 / 64, in1=rstd[:], op0=ALU.mult, op1=ALU.subtract),
                 reads=sk2 + ['rstd'], writes=['rstd'])
            S.op('dve', lambda e: e.tensor_scalar(out=rstd[:], in0=rstd[:], scalar1=64e-5, scalar2=None, op0=ALU.add), reads=['rstd'], writes=['rstd'])
            S.op('act', lambda e: e.sqrt(out=rstd[:], in_=rstd[:]), reads=['rstd'], writes=['rstd'])
            S.op('dve', lambda e: e.reciprocal(out=rstd[:], in_=rstd[:]), reads=['rstd'], writes=['rstd'])
            for p in range(NP):
                r = p % NR
                S.op('dve', lambda e: e.tensor_scalar(out=yb[r][:], in0=oslots[p], scalar1=mean[:, p:p + 1], scalar2=rstd[:, p:p + 1],
                                                      op0=ALU.subtract, op1=ALU.mult), reads=[okeys[p], 'mean', 'rstd'], writes=['yb%d' % r])
                S.op('pool', lambda e: e.tensor_tensor(out=yb[r][:], in0=yb[r][:], in1=lnr[:, 0, p * 128:(p + 1) * 128], op=ALU.mult),
                     reads=['yb%d' % r] + lnk, writes=['yb%d' % r])
                S.op('pool', lambda e: e.tensor_tensor(out=yb[r][:], in0=yb[r][:], in1=lnr[:, 1, p * 128:(p + 1) * 128], op=ALU.add),
                     reads=['yb%d' % r] + lnk, writes=['yb%d' % r])
                S.op('dve', lambda e: e.scalar_tensor_tensor(out=yb[r][:], in0=Vst[:, c, p, :], scalar=rkc[:, p:p + 1], in1=yb[r][:],
                                                             op0=ALU.mult, op1=ALU.add), reads=[('Vst', p, 0), ('Vst', p, 1), 'rkc', 'yb%d' % r], writes=['yb%d' % r])
                S.op('pool', lambda e: e.tensor_tensor(out=OG[:, c, p, :], in0=yb[r][:], in1=Gst[:, c, p, :], op=ALU.mult),
                     reads=['yb%d' % r, ('Gst', p, 0), ('Gst', p, 1)], writes=[('OG', c, p)])
        dst = og[t0:t0 + n, 0:512].rearrange("(c s) (p two n) -> s c p two n", s=64, two=2, n=64)
        ogk = [('OG', c, p) for c in range(nch) for p in range(NP)]
        for p in range(NP):
            S.dma('sp', dst[:, :, p, 0, :], OG[0:64, 0:nch, p, 0:64], reads=ogk, writes=['og'])
            S.dma('sp', dst[:, :, p, 1, :], OG[64:128, 0:nch, p, 64:128], reads=ogk, writes=['og'])
    barrier(S)
    A.close()


def sched_allgather(S, src, dst, reads=(), writes=(), ncores=8):
    q = 'pool'
    reads, writes = S._split(reads, writes)
    for t in S._deps(reads, writes):
        S._wait(q, t)
    s = S.drr
    S.drr = (S.drr + 1) % len(S.dsem)
    if S.dcnt[s] > 0:
        S._wait(q, (S.dsem[s], 16 * S.dcnt[s]))
    inst = S.nc.gpsimd.collective_compute("AllGather", mybir.AluOpType.bypass, replica_groups=[list(range(ncores))],
                                          ins=[src], outs=[dst])
    S.dcnt[s] += 1
    inst.then_inc(S.dsem[s], 16)
    tok = (S.dsem[s], 16 * S.dcnt[s])
    S._update(tok, reads, writes)
    return tok


D_MODEL = 4096
N_META = 16
SEQ = 16384
NPAD = 112
LP = NPAD + N_META + SEQ
NCORES = 8
NTB = 17


def _consts_common():
    return {"ident": np.eye(128, dtype=np.float32)}


def _hgrn_masks():
    m = np.zeros((3, 128, 512), np.float32)
    m[0] = 1.0
    m[0][:, ::64] = 0.0
    s = np.arange(64)[:, None]
    t = np.arange(64)[None, :]
    m[1][:64, :64] = ((s // 32 == t // 32) & (s <= t))
    m[2][:64, :64] = ((s < 32) & (t >= 32))
    return m


def _attn_consts(Lp):
    inv_freq = (500000.0 ** (-np.arange(0, 32, 2, dtype=np.float32) / 32)).astype(np.float32)
    pos = (np.arange(Lp) - NPAD).astype(np.float32)
    ang = pos[:, None] * inv_freq[None, :]
    rope = np.zeros((2, 128, Lp), np.float32)
    rope[0] = 1.0
    rope[0, 0:16] = np.cos(ang).T
    rope[0, 16:32] = np.cos(ang).T
    rope[1, 0:16] = -np.sin(ang).T
    rope[1, 16:32] = np.sin(ang).T
    perm = np.zeros((128, 128), np.float32)
    for m in range(16):
        perm[m + 16, m] = 1.0
        perm[m, m + 16] = 1.0
    am = np.zeros((4, 128, 512), np.float32)
    kk = np.arange(128)[:, None]
    qq = np.arange(128)[None, :]
    diag = ((kk // 64) <= (qq // 64)).astype(np.float32)
    for i in range(4):
        for j in range(4):
            if j > i:
                am[i][:, j * 128:(j + 1) * 128] = 1.0
            elif j == i:
                am[i][:, j * 128:(j + 1) * 128] = diag
    pb = np.zeros((128, 1), np.float32)
    pb[:NPAD] = -30000.0
    return rope, perm, am, pb


def _rwkv_consts():
    cm = np.zeros((6, 128, 128), np.float32)
    cm[0] = np.eye(128)
    r = np.arange(128)[:, None]
    c = np.arange(128)[None, :]
    same = (r // 64) == (c // 64)
    cm[1] = same
    cm[2] = same & (r < c)
    cm[3] = same & (r <= c)
    cm[4] = same & (c < r)
    return cm


def build_layerA(kind, layer_idx, j, Lp=None, D=D_MODEL):
    Lp = Lp or LP
    nc = bass.Bass("TRN2", target_bir_lowering=False)
    dt = lambda name, shape, k="ExternalInput", t=F32: nc.dram_tensor(name, shape, t, kind=k).ap()
    h = dt("h", [Lp, D])
    g = dt("g", [1, D])
    ident = dt("ident", [128, 128])
    og = dt("og", [Lp, 512], "ExternalOutput")
    uT_scr = dt("uT_scr", [128, D // 128, Lp + 1], "Internal", BF16)
    W1 = dt("W1", [D, 1024])
    W2 = dt("W2", [D, 1024])
    nzf = 1280 if kind == 'rwkv' else 1024
    zF = dt("zF", [nzf, Lp], "Internal")
    zT = dt("zT", [Lp, 1024], "Internal")
    S = Sched(nc)
    emit_prep(S, nc, Lp=Lp, D=D, h=h, g=g, uT_scr=uT_scr, ident=ident)
    if kind == 'rwkv':
        mu = dt("mu", [6, D])
        W3 = dt("W3", [D, 256])
        t1 = [('F', 0, c * 128, c * 128) for c in range(4)] + [('F', 1, 512 + c * 128, 512 + c * 128) for c in range(4)]
        emit_mm_pass(S, nc, Lp=Lp, D=D, uT_scr=uT_scr, W=W1, ncols=1024, tiles=t1, zF=zF, zT=zT, mu=mu, nvar=6)
        t2 = [('T', 2, 0, 0), ('T', 3, 512, 512)]
        emit_mm_pass(S, nc, Lp=Lp, D=D, uT_scr=uT_scr, W=W2, ncols=1024, tiles=t2, zF=zF, zT=zT, mu=mu, nvar=6)
        t3 = [('F', 4, 0, 1024), ('F', 5, 128, 1152)]
        emit_mm_pass(S, nc, Lp=Lp, D=D, uT_scr=uT_scr, W=W3, ncols=256, tiles=t3, zF=zF, zT=zT, mu=mu, nvar=6)
        w2 = dt("w2", [128, 512])
        a2 = dt("a2", [128, 512])
        vecs = dt("vecs", [128, 5, 4])
        lnx = dt("lnx", [2, 512])
        cmat = dt("cmat", [6, 128, 128])
        emit_rwkv(S, nc, Lp=Lp, zF=zF, zT=zT, w2=w2, a2=a2, vecs=vecs, lnx=lnx, og=og, cmat=cmat)
    else:
        t1 = [('F', -1, c * 128, c * 128) for c in range(8)]
        emit_mm_pass(S, nc, Lp=Lp, D=D, uT_scr=uT_scr, W=W1, ncols=1024, tiles=t1, zF=zF, zT=zT)
        t2 = [('T', -1, 0, 0), ('T', -1, 512, 512)]
        emit_mm_pass(S, nc, Lp=Lp, D=D, uT_scr=uT_scr, W=W2, ncols=1024, tiles=t2, zF=zF, zT=zT)
        if kind == 'hgrn2':
            lbl = dt("lbl", [128, 2, 4])
            gn = dt("gn", [1, 128])
            masks = dt("masks", [3, 128, 512])
            emit_hgrn2(S, nc, Lp=Lp, zF=zF, zT=zT, lbl=lbl, n_la=2, layer_j=j, gn=gn, og=og, ident=ident, masks=masks, NH=4)
        else:
            lamv = dt("lamv", [4, 128])
            sublng = dt("sublng", [1, 256])
            rope = dt("rope", [2, 128, Lp])
            perm = dt("perm", [128, 128])
            amask = dt("amask", [4, 128, 512])
            padbias = dt("padbias", [128, 1])
            lambda_init = 0.8 - 0.6 * float(np.exp(-0.3 * layer_idx))
            emit_attn(S, nc, Lp=Lp, zF=zF, zT=zT, lamv=lamv, lambda_init=lambda_init, sublng=sublng, og=og, rope=rope, perm=perm,
                      amask=amask, padbias=padbias, NH=2, npad=NPAD)
    S.finish('sp')
    S.close()
    return nc


def build_layerB(NT=None, D=D_MODEL):
    NT = NT or NTB
    nc = bass.Bass("TRN2", target_bir_lowering=False)
    dt = lambda name, shape, k="ExternalInput", t=F32: nc.dram_tensor(name, shape, t, kind=k).ap()
    L = NT * 128
    ogc = dt("ogc", [L, D])
    hc = dt("hc", [L, D])
    g = dt("g", [1, D])
    ident = dt("ident", [128, 128])
    Wo = [dt("Wo%d" % i, [D, 1024]) for i in range(4)]
    hn = dt("hn", [L, D], "ExternalOutput")
    oT_scr = dt("oT_scr", [128, D // 128, L + 1], "Internal", BF16)
    y_scr = dt("y_scr", [L, D], "Internal")
    zdummy = dt("zdummy", [128, 128], "Internal")
    S = Sched(nc)
    emit_prep(S, nc, Lp=L, D=D, h=ogc, g=g, uT_scr=oT_scr, ident=ident, norm=False)
    for i in range(4):
        tl = [('T', -1, 0, i * 1024), ('T', -1, 512, i * 1024 + 512)]
        emit_mm_pass(S, nc, Lp=L, D=D, uT_scr=oT_scr, W=Wo[i], ncols=1024, tiles=tl, zF=zdummy, zT=y_scr)
    emit_postnorm(S, nc, NT=NT, D=D, y=y_scr, h=hc, g=g, out=hn)
    S.finish('sp')
    S.close()
    return nc


_NC_CACHE = {}


def _get_nc(key, fn):
    if key not in _NC_CACHE:
        _NC_CACHE[key] = fn()
    return _NC_CACHE[key]


def _f32(a):
    return np.ascontiguousarray(a, dtype=np.float32)


def _layerA_inmaps(kind, layer_idx, j, h, P):
    maps = []
    g = _f32(P['pre_norm_g'][layer_idx][None, :])
    ident = np.eye(128, dtype=np.float32)
    d = D_MODEL
    if kind == 'attn':
        rope, perm, am, pb = _attn_consts(LP)
    for c in range(NCORES):
        cs = slice(c * 512, (c + 1) * 512)
        m = {"h": h, "g": g, "ident": ident}
        if kind == 'hgrn2':
            w = P['a_w_in'][j]
            m["W1"] = _f32(np.concatenate([w[:, cs], w[:, d + c * 512:d + (c + 1) * 512]], 1))
            m["W2"] = _f32(np.concatenate([w[:, 2 * d + c * 512:2 * d + (c + 1) * 512], w[:, 3 * d + c * 512:3 * d + (c + 1) * 512]], 1))
            lg = P['a_lb_logits'][:, cs]
            m["lbl"] = _f32(lg.reshape(2, 4, 128).transpose(2, 0, 1))
            m["gn"] = _f32(P['a_gnorm_g'][j][None, :])
            m["masks"] = _hgrn_masks()
        elif kind == 'attn':
            w = P['b_w_in'][j]
            m["W1"] = _f32(np.concatenate([w[:, cs], w[:, d + c * 512:d + (c + 1) * 512]], 1))
            m["W2"] = _f32(np.concatenate([w[:, 2 * d + c * 512:2 * d + (c + 1) * 512], w[:, 3 * d + c * 512:3 * d + (c + 1) * 512]], 1))
            m["lamv"] = _f32(np.stack([P['b_lam_q1'][j], P['b_lam_k1'][j], P['b_lam_q2'][j], P['b_lam_k2'][j]]))
            m["sublng"] = _f32(P['b_subln_g'][j][None, :])
            m["rope"] = rope
            m["perm"] = perm
            m["amask"] = am
            m["padbias"] = pb
        else:
            w = P['c_w_in'][j]
            m["W1"] = _f32(np.concatenate([w[:, cs], w[:, d + c * 512:d + (c + 1) * 512]], 1))
            m["W2"] = _f32(np.concatenate([w[:, 2 * d + c * 512:2 * d + (c + 1) * 512], w[:, 3 * d + c * 512:3 * d + (c + 1) * 512]], 1))
            m["W3"] = _f32(np.concatenate([P['c_w1'][j], P['c_a1'][j]], 1))
            m["mu"] = _f32(P['c_mu'][j])
            m["w2"] = _f32(P['c_w2'][j][:, cs])
            m["a2"] = _f32(P['c_a2'][j][:, cs])
            vs = [P['c_w0'][j], P['c_a0'][j], P['c_k_k'][j], P['c_k_a'][j], P['c_r_k'][j].reshape(-1)]
            m["vecs"] = _f32(np.stack([v[cs].reshape(4, 128).T for v in vs], 1))
            m["lnx"] = _f32(np.stack([P['c_lnx_w'][j][cs], P['c_lnx_b'][j][cs]]))
            m["cmat"] = _rwkv_consts()
        maps.append(m)
    return maps


def _tok_rows(c):
    per = (LP - 128) // NCORES
    return np.concatenate([np.arange(0, 128), np.arange(128 + c * per, 128 + (c + 1) * per)])


def kernel(**inputs):
    P = {k: np.asarray(v) for k, v in inputs.items()}
    x = P['x']
    h = np.zeros((LP, D_MODEL), np.float32)
    h[NPAD:NPAD + N_META] = P['meta_tokens']
    h[NPAD + N_META:] = x[0]
    kinds = ['hgrn2', 'attn', 'rwkv']
    ident = np.eye(128, dtype=np.float32)
    for i in range(4):
        kind = kinds[i % 3]
        j = i // 3
        ncA = _get_nc(('A', kind, i, j), lambda: build_layerA(kind, i, j))
        resA = run_bass_kernel_spmd(ncA, _layerA_inmaps(kind, i, j, h, P), core_ids=list(range(NCORES)))
        og = np.concatenate([r["og"] for r in resA.results], axis=1)
        ncB = _get_nc(('B',), build_layerB)
        wo = {'hgrn2': P['a_w_out'], 'attn': P['b_w_out'], 'rwkv': P['c_w_out']}[kind][j]
        wos = {"Wo%d" % q: _f32(wo[:, q * 1024:(q + 1) * 1024]) for q in range(4)}
        gB = _f32(P['post_norm_g'][i][None, :])
        mapsB = []
        for c in range(NCORES):
            rows = _tok_rows(c)
            m = {"ogc": _f32(og[rows]), "hc": _f32(h[rows]), "g": gB, "ident": ident}
            m.update(wos)
            mapsB.append(m)
        resB = run_bass_kernel_spmd(ncB, mapsB, core_ids=list(range(NCORES)))
        hn = np.empty_like(h)
        hn[0:128] = resB.results[0]["hn"][0:128]
        per = (LP - 128) // NCORES
        for c in range(NCORES):
            hn[128 + c * per:128 + (c + 1) * per] = resB.results[c]["hn"][128:]
        h = hn
    return np.ascontiguousarray(h[NPAD + N_META:][None]).astype(np.float32)
```

```python
import numpy as np
import concourse.bass as bass
import concourse.mybir as mybir
from concourse.bass_utils import run_bass_kernel_spmd

F32 = mybir.dt.float32
BF16 = mybir.dt.bfloat16
ALU = mybir.AluOpType
AF = mybir.ActivationFunctionType
AX = mybir.AxisListType


class Sched:
    def __init__(self, nc, n_dma_sems=24):
        self.nc = nc
        self.eng = {'pe': nc.tensor, 'act': nc.scalar, 'dve': nc.vector,
                    'pool': nc.gpsimd, 'sp': nc.sync}
        self.prog = {}
        self.cnt = {}
        self.known = {e: {} for e in self.eng}
        self.same_sync = {'pe': False, 'act': True, 'dve': True, 'pool': True, 'sp': True}
        self.res = {}
        self._ctx = []
        for e in ('pe', 'act', 'dve', 'pool'):
            cm = nc.semaphore('prog_' + e)
            self.prog[e] = cm.__enter__()
            self._ctx.append(cm)
            self.cnt[e] = 0
        self.dsem = []
        self.dcnt = []
        for i in range(n_dma_sems):
            cm = nc.semaphore('dma_%d' % i)
            self.dsem.append(cm.__enter__())
            self._ctx.append(cm)
            self.dcnt.append(0)
        self.drr = 0
        self.sem_name = {}
        self.n_wait = 0

    def close(self):
        for cm in reversed(self._ctx):
            cm.__exit__(None, None, None)

    def _wait(self, e, tok):
        sem, val = tok
        k = self.known[e]
        key = id(sem)
        if k.get(key, 0) >= val:
            return
        self.eng[e].wait_ge(sem, val)
        self.n_wait += 1
        k[key] = val

    @staticmethod
    def _is_excl(k):
        if isinstance(k, tuple):
            k = k[0]
        return isinstance(k, str) and k.startswith('p_')

    def _split(self, reads, writes):
        reads = list(reads)
        writes = list(writes)
        ex = [k for k in reads if self._is_excl(k)]
        if ex:
            reads = [k for k in reads if not self._is_excl(k)]
            writes = writes + [k for k in ex if k not in writes]
        return reads, writes

    def _deps(self, reads, writes):
        toks = {}

        def add(t):
            if t is None:
                return
            key = id(t[0])
            if key not in toks or toks[key][1] < t[1]:
                toks[key] = t
        for k in reads:
            r = self.res.get(k)
            if r:
                add(r['w'])
        for k in writes:
            r = self.res.get(k)
            if r:
                add(r['w'])
                for t in r['r'].values():
                    add(t)
        return list(toks.values())

    def _update(self, tok, reads, writes):
        for k in reads:
            r = self.res.setdefault(k, {'w': None, 'r': {}})
            key = id(tok[0])
            if key not in r['r'] or r['r'][key][1] < tok[1]:
                r['r'][key] = tok
        for k in writes:
            self.res[k] = {'w': tok, 'r': {}}

    def op(self, e, fn, reads=(), writes=()):
        reads, writes = self._split(reads, writes)
        for t in self._deps(reads, writes):
            if t[0] is self.prog.get(e) and not self.same_sync[e]:
                continue
            self._wait(e, t)
        inst = fn(self.eng[e])
        self.cnt[e] += 1
        inst.then_inc(self.prog[e], 1)
        tok = (self.prog[e], self.cnt[e])
        self._update(tok, reads, writes)
        return tok

    def dma(self, q, out, in_, reads=(), writes=(), **kw):
        reads, writes = self._split(reads, writes)
        for t in self._deps(reads, writes):
            self._wait(q, t)
        s = self.drr
        self.drr = (self.drr + 1) % len(self.dsem)
        if self.dcnt[s] > 0:
            self._wait(q, (self.dsem[s], 16 * self.dcnt[s]))
        inst = self.eng[q].dma_start(out=out, in_=in_, **kw)
        self.dcnt[s] += 1
        inst.then_inc(self.dsem[s], 16)
        tok = (self.dsem[s], 16 * self.dcnt[s])
        self._update(tok, reads, writes)
        return tok

    def finish(self, e='sp'):
        for s in range(len(self.dsem)):
            if self.dcnt[s] > 0:
                self._wait(e, (self.dsem[s], 16 * self.dcnt[s]))
        for x in ('pe', 'act', 'dve', 'pool'):
            if self.cnt[x] > 0:
                self._wait(e, (self.prog[x], self.cnt[x]))


class Alloc:
    _uid = [0]

    def __init__(self, nc):
        self.nc = nc
        self._ctx = []
        Alloc._uid[0] += 1
        self.sfx = "_a%d" % Alloc._uid[0]

    def sb(self, name, shape, dt):
        cm = self.nc.sbuf_tensor(name + self.sfx, shape, dt)
        t = cm.__enter__()
        self._ctx.append(cm)
        return t

    def ps(self, name, shape, dt):
        cm = self.nc.psum_tensor(name + self.sfx, shape, dt)
        t = cm.__enter__()
        self._ctx.append(cm)
        return t

    def close(self):
        for cm in reversed(self._ctx):
            cm.__exit__(None, None, None)
        self._ctx = []


def barrier(S):
    toks = []
    for s in range(len(S.dsem)):
        if S.dcnt[s] > 0:
            toks.append((S.dsem[s], 16 * S.dcnt[s]))
    for x in ('pe', 'act', 'dve', 'pool'):
        if S.cnt[x] > 0:
            toks.append((S.prog[x], S.cnt[x]))
    for e in ('pe', 'act', 'dve', 'pool', 'sp'):
        for t in toks:
            S._wait(e, t)
    S.res = {}


def emit_prep(S, nc, *, Lp, D, h, g, uT_scr, ident, eps=1e-6, norm=True):
    KC = D // 128
    NTILE = Lp // 128
    A = Alloc(nc)
    hb = [A.sb("hb%d" % i, [128, D], F32) for i in range(2)]
    ub = [A.sb("ub%d" % i, [128, D], BF16) for i in range(2)]
    junk = A.sb("junk", [128, D], BF16)
    gb = A.sb("gb", [128, D], F32)
    ss = [A.sb("ss%d" % i, [128, 1], F32) for i in range(2)]
    rs = [A.sb("rs%d" % i, [128, 1], F32) for i in range(2)]
    idf = A.sb("idf", [128, 128], F32)
    idb = A.sb("idb", [128, 128], BF16)
    uo = [A.sb("uo%d" % i, [128, KC, 128], BF16) for i in range(2)]
    zc = A.sb("zc", [128, KC, 1], BF16)
    G = min(8, KC)
    pT = [A.ps("pT%d" % i, [128, G, 128], BF16) for i in range(2)]
    S.dma('sp', idf[:], ident[:, :], writes=['idf'])
    S.op('dve', lambda e: e.tensor_copy(out=idb[:], in_=idf[:]), reads=['idf'], writes=['idb'])
    if norm:
        S.dma('sp', gb[:], g[0:1, :].partition_broadcast(128), writes=['gb'])
    S.op('pool', lambda e: e.memset(zc[:], 0.0), writes=['zc'])
    S.dma('sp', uT_scr[:, :, 0:1], zc[:], reads=['zc'], allow_slow_non_contiguous=True)
    ev = 0
    for j in range(NTILE):
        sl = j % 2
        S.dma('sp', hb[sl][:], h[j * 128:(j + 1) * 128, :], writes=['hb%d' % sl])
        if norm:
            S.op('dve', lambda e: e.memset(ss[sl][:], 0.0), writes=['ss%d' % sl])
            S.op('act', lambda e: e.activation(out=junk[:], in_=hb[sl][:], func=AF.Square, accum_out=ss[sl][:]),
                 reads=['hb%d' % sl], writes=['junk', 'ss%d' % sl])
            S.op('dve', lambda e: e.tensor_scalar(out=rs[sl][:], in0=ss[sl][:], scalar1=1.0 / D, scalar2=eps,
                                                  op0=ALU.mult, op1=ALU.add), reads=['ss%d' % sl], writes=['rs%d' % sl])
            S.op('act', lambda e: e.sqrt(out=rs[sl][:], in_=rs[sl][:]), reads=['rs%d' % sl], writes=['rs%d' % sl])
            S.op('dve', lambda e: e.reciprocal(out=rs[sl][:], in_=rs[sl][:]), reads=['rs%d' % sl], writes=['rs%d' % sl])
            S.op('dve', lambda e: e.scalar_tensor_tensor(out=ub[sl][:], in0=hb[sl][:], scalar=rs[sl][:, 0:1], in1=gb[:],
                                                         op0=ALU.mult, op1=ALU.mult),
                 reads=['hb%d' % sl, 'rs%d' % sl, 'gb'], writes=['ub%d' % sl])
        else:
            S.op('dve', lambda e: e.tensor_copy(out=ub[sl][:], in_=hb[sl][:]), reads=['hb%d' % sl], writes=['ub%d' % sl])
        for k0 in range(0, KC, G):
            pb = ev % 2
            for k in range(k0, k0 + G):
                S.op('pe', lambda e, k=k: e.transpose(out=pT[pb][:, k - k0, :], in_=ub[sl][:, k * 128:(k + 1) * 128],
                                                      identity=idb[:]),
                     reads=['ub%d' % sl, 'idb'], writes=['p_T%d' % pb])
            eng = 'act' if ev % 2 == 0 else 'dve'
            if eng == 'act':
                S.op('act', lambda e: e.copy(out=uo[sl][:, k0:k0 + G, :], in_=pT[pb][:]),
                     reads=['p_T%d' % pb], writes=[('uo', sl, k0)])
            else:
                S.op('dve', lambda e: e.tensor_copy(out=uo[sl][:, k0:k0 + G, :], in_=pT[pb][:]),
                     reads=['p_T%d' % pb], writes=[('uo', sl, k0)])
            ev += 1
        S.dma('act', uT_scr[:, :, 1 + j * 128:1 + (j + 1) * 128], uo[sl][:],
              reads=[('uo', sl, k0) for k0 in range(0, KC, G)], writes=['uT_scr'])
    barrier(S)
    A.close()


def emit_mm_pass(S, nc, *, Lp, D, uT_scr, W, ncols, tiles, zF, zT, mu=None, nvar=1, TB=512):
    KC = D // 128
    A = Alloc(nc)
    Wb = A.sb("Wb", [128, KC, ncols], BF16)
    SW = min(ncols, 512)
    Wst = [A.sb("Wst%d" % i, [128, SW], F32) for i in range(2)]
    use_var = mu is not None
    NUB = 1 if use_var else 2
    uT = [A.sb("uT%d" % i, [128, KC, TB + 1], BF16) for i in range(NUB)]
    st = [A.sb("st%d" % i, [128, 512], F32) for i in range(4)]
    pz = [A.ps("pz%d" % i, [128, 512], F32) for i in range(4)]
    if use_var:
        muT = A.sb("muT", [128, nvar, KC], F32)
        dT = A.sb("dT", [128, KC, TB], BF16)
        mx = [A.sb("mx%d" % i, [128, KC, TB], BF16) for i in range(2)]
        S.dma('sp', muT[:], mu.rearrange("v (k p) -> p v k", p=128), writes=['muT'], allow_slow_non_contiguous=True)
    i = 0
    for k in range(KC):
        for c0 in range(0, ncols, SW):
            sl = i % 2
            S.dma('sp', Wst[sl][:, 0:min(SW, ncols - c0)], W[k * 128:(k + 1) * 128, c0:c0 + min(SW, ncols - c0)],
                  writes=['Wst%d' % sl])
            S.op('pool', lambda e, k=k, c0=c0, sl=sl: e.tensor_copy(out=Wb[:, k, c0:c0 + min(SW, ncols - c0)],
                                                                   in_=Wst[sl][:, 0:min(SW, ncols - c0)]),
                 reads=['Wst%d' % sl], writes=[('Wb', k)])
            i += 1
    Wkeys = [('Wb', k) for k in range(KC)]
    nblk = (Lp + TB - 1) // TB
    oc = 0
    vars_used = sorted(set(t[1] for t in tiles))
    mxc = 0
    for b in range(nblk):
        t0 = b * TB
        n = min(TB, Lp - t0)
        bs = b % NUB
        S.dma('sp', uT[bs][:, :, 0:n + 1], uT_scr[:, :, t0:t0 + n + 1], writes=['uT%d' % bs])
        if use_var:
            for k in range(KC):
                eng = 'pool'
                S.op(eng, lambda e, k=k: e.tensor_tensor(out=dT[:, k, 0:n], in0=uT[bs][:, k, 0:n], in1=uT[bs][:, k, 1:n + 1],
                                                         op=ALU.subtract),
                     reads=['uT%d' % bs], writes=[('dT', k)])
        for v in vars_used:
            if v >= 0:
                ms = mxc % 2
                mxc += 1
                for k in range(KC):
                    eng = 'dve'
                    S.op(eng, lambda e, k=k, v=v, ms=ms: e.scalar_tensor_tensor(
                        out=mx[ms][:, k, 0:n], in0=dT[:, k, 0:n], scalar=muT[:, v, k:k + 1], in1=uT[bs][:, k, 1:n + 1],
                        op0=ALU.mult, op1=ALU.add),
                        reads=[('dT', k), 'muT', 'uT%d' % bs], writes=[('mx', ms, k)])
                src = lambda k, a, bb, ms=ms: mx[ms][:, k, a:bb]
                skeys = [('mx', ms, k) for k in range(KC)]
            else:
                src = lambda k, a, bb: uT[bs][:, k, 1 + a:1 + bb]
                skeys = ['uT%d' % bs]
            for (kind, var, col0, z0) in tiles:
                if var != v:
                    continue
                if kind == 'F':
                    pb = oc % 4
                    oc += 1
                    for k in range(KC):
                        S.op('pe', lambda e, k=k: e.matmul(pz[pb][:, 0:n], lhsT=Wb[:, k, col0:col0 + 128], rhs=src(k, 0, n),
                                                           start=(k == 0), stop=(k == KC - 1)),
                             reads=skeys + Wkeys, writes=['p_z%d' % pb])
                    if pb % 2 == 0:
                        S.op('act', lambda e: e.copy(out=st[pb][:, 0:n], in_=pz[pb][:, 0:n]), reads=['p_z%d' % pb], writes=['st%d' % pb])
                    else:
                        S.op('dve', lambda e: e.tensor_copy(out=st[pb][:, 0:n], in_=pz[pb][:, 0:n]), reads=['p_z%d' % pb], writes=['st%d' % pb])
                    S.dma('act' if pb % 2 == 0 else 'pool', zF[z0:z0 + 128, t0:t0 + n], st[pb][:, 0:n], reads=['st%d' % pb], writes=['z'])
                else:
                    for jt in range(n // 128):
                        pb = oc % 4
                        oc += 1
                        for k in range(KC):
                            S.op('pe', lambda e, k=k: e.matmul(pz[pb][:, :], lhsT=src(k, jt * 128, (jt + 1) * 128),
                                                               rhs=Wb[:, k, col0:col0 + 512],
                                                               start=(k == 0), stop=(k == KC - 1)),
                                 reads=skeys + Wkeys, writes=['p_z%d' % pb])
                        if pb % 2 == 0:
                            S.op('act', lambda e: e.copy(out=st[pb][:], in_=pz[pb][:]), reads=['p_z%d' % pb], writes=['st%d' % pb])
                        else:
                            S.op('dve', lambda e: e.tensor_copy(out=st[pb][:], in_=pz[pb][:]), reads=['p_z%d' % pb], writes=['st%d' % pb])
                        S.dma('act' if pb % 2 == 0 else 'pool', zT[t0 + jt * 128:t0 + (jt + 1) * 128, z0:z0 + 512], st[pb][:], reads=['st%d' % pb], writes=['z'])
    barrier(S)
    A.close()


def emit_hgrn2(S, nc, *, Lp, zF, zT, lbl, n_la, layer_j, gn, og, ident, masks, TB=512, NH=4):
    HW = NH * 128
    A = Alloc(nc)
    f32 = lambda name, shape: A.sb(name, shape, F32)
    bf = lambda name, shape: A.sb(name, shape, BF16)
    W = NH * TB
    qT = f32("qT", [128, NH, TB]); fT = f32("fT", [128, NH, TB]); kk = f32("kk", [128, NH, TB])
    bb = f32("bb", [128, NH, TB]); dd = f32("dd", [128, NH, TB]); EE = f32("EE", [128, NH, TB])
    smask = f32("smask", [128, NH, TB])
    prod = {nm: bf("p_" + nm, [128, NH, TB]) for nm in ('qd', 'kd', 'qo', 'ko', 'qh', 'kh')}
    NCH = TB // 64
    tm = f32("tm", [64, NCH, 2 * HW])
    vb = bf("vb", [64, NCH, HW])
    g2 = f32("g2", [64, NCH, HW])
    gnb = f32("gnb", [64, HW])
    ebl = f32("ebl", [128, NH, NCH])
    lb_t = f32("lb_t", [128, n_la, NH]); lb = f32("lb", [128, NH]); oml = f32("oml", [128, NH])
    lsum = f32("lsum", [128, NH]); lcum = f32("lcum", [128, NH])
    idf = f32("idf", [128, 128]); idb = bf("idb", [128, 128])
    md = f32("md", [64, 64]); mo = f32("mo", [64, 64])
    St = [f32("St%d" % h, [128, 128]) for h in range(NH)]
    Sb = [bf("Sb%d" % h, [128, 128]) for h in range(NH)]
    ktok = [bf("ktok%d" % i, [64, 128]) for i in range(NH)]
    scA = [f32("scA%d" % i, [64, 64]) for i in range(NH)]
    scB = [f32("scB%d" % i, [64, 64]) for i in range(NH)]
    scT = [bf("scT%d" % i, [64, 64]) for i in range(NH)]
    sq = f32("sq", [64, NH, 128])
    ss = [f32("ssq%d" % i, [64, NH]) for i in range(2)]
    ogt = [f32("ogt%d" % i, [64, HW]) for i in range(2)]
    p_s = [A.ps("p_s%d" % i, [64, 2, 64], F32) for i in range(2)]
    p_sd = [p_s[i][:, 0, :] for i in range(2)]
    p_so = [p_s[i][:, 1, :] for i in range(2)]
    p_kt = [A.ps("p_kt%d" % i, [64, 128], BF16) for i in range(2)]
    p_o = [A.ps("p_o%d" % i, [64, NH, 128], F32) for i in range(2)]
    p_ds = [A.ps("p_ds%d" % i, [128, 128], F32) for i in range(2)]

    S.dma('sp', idf[:], ident[:, :], writes=['idf'])
    S.op('dve', lambda e: e.tensor_copy(out=idb[:], in_=idf[:]), reads=['idf'], writes=['idb'])
    for hh in range(NH):
        S.dma('sp', smask[:, hh, :], masks[0, :, 0:TB], writes=[('smask', hh)])
    S.dma('sp', md[:], masks[1, 0:64, 0:64], writes=['md'])
    S.dma('sp', mo[:], masks[2, 0:64, 0:64], writes=['mo'])
    for hh in range(NH):
        S.dma('sp', gnb[:, hh * 128:(hh + 1) * 128], gn[0:1, :].partition_broadcast(64), writes=[('gnb', hh)])
    gnkeys = [('gnb', hh) for hh in range(NH)]
    smkeys = [('smask', hh) for hh in range(NH)]
    S.dma('sp', lb_t[:], lbl[:, :, :], writes=['lb_t'])
    S.op('act', lambda e: e.activation(out=lb_t[:], in_=lb_t[:], func=AF.Exp), reads=['lb_t'], writes=['lb_t'])
    S.op('dve', lambda e: e.tensor_copy(out=lsum[:], in_=lb_t[:, 0, :]), reads=['lb_t'], writes=['lsum'])
    for j in range(1, n_la):
        S.op('dve', lambda e, j=j: e.tensor_tensor(out=lsum[:], in0=lsum[:], in1=lb_t[:, j, :], op=ALU.add),
             reads=['lb_t', 'lsum'], writes=['lsum'])
    S.op('dve', lambda e: e.reciprocal(out=lsum[:], in_=lsum[:]), reads=['lsum'], writes=['lsum'])
    S.op('dve', lambda e: e.memset(lcum[:], 0.0), writes=['lcum'])
    for j in range(0, layer_j + 1):
        S.op('dve', lambda e, j=j: e.tensor_tensor(out=lcum[:], in0=lcum[:], in1=lb_t[:, j, :], op=ALU.add),
             reads=['lb_t', 'lcum'], writes=['lcum'])
    S.op('dve', lambda e: e.tensor_tensor(out=lcum[:], in0=lcum[:], in1=lb_t[:, 0, :], op=ALU.subtract),
         reads=['lb_t', 'lcum'], writes=['lcum'])
    S.op('dve', lambda e: e.tensor_tensor(out=lb[:], in0=lcum[:], in1=lsum[:], op=ALU.mult),
         reads=['lsum', 'lcum'], writes=['lb'])
    S.op('dve', lambda e: e.tensor_scalar(out=oml[:], in0=lb[:], scalar1=-1.0, scalar2=1.0, op0=ALU.mult, op1=ALU.add),
         reads=['lb'], writes=['oml'])
    for h in range(NH):
        S.op('dve', lambda e, h=h: e.memset(St[h][:], 0.0), writes=['St%d' % h])
        S.op('pool', lambda e, h=h: e.memset(Sb[h][:], 0.0), writes=['Sb%d' % h])

    nblk = (Lp + TB - 1) // TB
    cc = 0
    for b in range(nblk):
        t0 = b * TB
        n = min(TB, Lp - t0)
        nch = n // 64
        S.dma('sp', qT[:, :, 0:n], zF[0:HW, t0:t0 + n].rearrange("(h p) t -> p h t", p=128), writes=['qT'])
        S.dma('sp', fT[:, :, 0:n], zF[HW:2 * HW, t0:t0 + n].rearrange("(h p) t -> p h t", p=128), writes=['fT'])
        S.dma('sp', tm[:, 0:nch, :], zT[t0:t0 + n, 0:2 * HW].rearrange("(c p) n -> p c n", p=64), writes=['tm'])
        S.op('act', lambda e: e.activation(out=fT[:, :, 0:n], in_=fT[:, :, 0:n], func=AF.Sigmoid), reads=['fT'], writes=['fT'])
        for h in range(NH):
            S.op('dve', lambda e, h=h: e.tensor_scalar(out=fT[:, h, 0:n], in0=fT[:, h, 0:n], scalar1=oml[:, h:h + 1],
                                                      scalar2=lb[:, h:h + 1], op0=ALU.mult, op1=ALU.add),
                 reads=['fT', 'oml', 'lb'], writes=['fT'])
        S.op('dve', lambda e: e.tensor_scalar(out=kk[:, :, 0:n], in0=fT[:, :, 0:n], scalar1=-1.0, scalar2=1.0,
                                              op0=ALU.mult, op1=ALU.add), reads=['fT'], writes=['kk'])
        S.op('act', lambda e: e.activation(out=fT[:, :, 0:n], in_=fT[:, :, 0:n], func=AF.Ln), reads=['fT'], writes=['fT'])
        for h in range(NH):
            S.op('dve', lambda e, h=h: e.tensor_tensor_scan(out=bb[:, h, 0:n], data0=smask[:, h, 0:n], data1=fT[:, h, 0:n],
                                                           initial=0.0, op0=ALU.mult, op1=ALU.add),
                 reads=['fT'] + smkeys, writes=['bb'])

        def factor(name, src, ref_w, ref_i, sign):
            if ref_w is None:
                S.op('act', lambda e: e.activation(out=EE[:, :, 0:n], in_=bb[:, :, 0:n], func=AF.Exp, scale=float(sign)),
                     reads=['bb'], writes=['EE'])
            else:
                for h in range(NH):
                    b3 = bb[:, h, 0:n].rearrange("p (c w) -> p c w", w=ref_w)
                    d3 = dd[:, h, 0:n].rearrange("p (c w) -> p c w", w=ref_w)
                    S.op('dve', lambda e, b3=b3, d3=d3: e.tensor_tensor(
                        out=d3, in0=b3, in1=b3[:, :, ref_i:ref_i + 1].to_broadcast([128, n // ref_w, ref_w]), op=ALU.subtract),
                        reads=['bb'], writes=['dd'])
                S.op('act', lambda e: e.activation(out=EE[:, :, 0:n], in_=dd[:, :, 0:n], func=AF.Exp, scale=float(sign)),
                     reads=['dd'], writes=['EE'])
            S.op('dve', lambda e: e.tensor_tensor(out=prod[name][:, :, 0:n], in0=src[:, :, 0:n], in1=EE[:, :, 0:n], op=ALU.mult),
                 reads=['EE', 'qT', 'kk'], writes=['pr_' + name])
        factor('qd', qT, 32, 15, 1.0)
        factor('kd', kk, 32, 15, -1.0)
        factor('qo', qT, 64, 31, 1.0)
        factor('ko', kk, 64, 31, -1.0)
        factor('qh', qT, None, None, 1.0)
        factor('kh', kk, 64, 63, -1.0)
        for h in range(NH):
            S.op('act', lambda e, h=h: e.activation(out=ebl[:, h, 0:nch],
                                                   in_=bb[:, h, 0:n].rearrange("p (c w) -> p c w", w=64)[:, :, 63],
                                                   func=AF.Exp), reads=['bb'], writes=['ebl'])
        S.op('act', lambda e: e.copy(out=vb[:, 0:nch, :], in_=tm[:, 0:nch, 0:HW]), reads=['tm'], writes=['vb'])
        S.op('act', lambda e: e.activation(out=g2[:, 0:nch, :], in_=tm[:, 0:nch, HW:2 * HW], func=AF.Silu), reads=['tm'], writes=['g2'])
        S.op('dve', lambda e: e.tensor_tensor(out=g2[:, 0:nch, :], in0=g2[:, 0:nch, :],
                                              in1=gnb[:].unsqueeze(1).to_broadcast([64, nch, HW]), op=ALU.mult),
             reads=['g2'] + gnkeys, writes=['g2'])
        for c in range(nch):
            cs = slice(c * 64, (c + 1) * 64)
            po = cc % 2
            def head_gen(h):
                i2 = (cc * NH + h) % 2
                S.op('pe', lambda e: e.matmul(p_sd[i2], lhsT=prod['kd'][:, h, cs], rhs=prod['qd'][:, h, cs], start=True, stop=True),
                     reads=['pr_kd', 'pr_qd'], writes=['p_s%d' % i2])
                S.op('pe', lambda e: e.matmul(p_so[i2], lhsT=prod['ko'][:, h, cs], rhs=prod['qo'][:, h, cs], start=True, stop=True),
                     reads=['pr_ko', 'pr_qo'], writes=['p_s%d' % i2])
                S.op('pe', lambda e: e.transpose(out=p_kt[i2][:], in_=prod['kh'][:, h, cs], identity=idb[:]),
                     reads=['pr_kh', 'idb'], writes=['p_kt%d' % i2])
                S.op('dve', lambda e: e.tensor_tensor(out=scA[h][:], in0=p_sd[i2], in1=md[:], op=ALU.mult),
                     reads=['p_s%d' % i2, 'md'], writes=['scA%d' % h])
                S.op('dve', lambda e: e.tensor_tensor(out=scB[h][:], in0=p_so[i2], in1=mo[:], op=ALU.mult),
                     reads=['p_s%d' % i2, 'mo'], writes=['scB%d' % h])
                S.op('dve', lambda e: e.tensor_tensor(out=scT[h][:], in0=scA[h][:], in1=scB[h][:], op=ALU.add),
                     reads=['scA%d' % h, 'scB%d' % h], writes=['scT%d' % h])
                S.op('act', lambda e: e.copy(out=ktok[h][:], in_=p_kt[i2][:]), reads=['p_kt%d' % i2], writes=['ktok%d' % h])
                yield
                S.op('pe', lambda e: e.matmul(p_o[po][:, h, :], lhsT=scT[h][:], rhs=vb[:, c, h * 128:(h + 1) * 128], start=True, stop=False),
                     reads=['scT%d' % h, 'vb'], writes=[('p_o', po)])
                S.op('pe', lambda e: e.matmul(p_o[po][:, h, :], lhsT=prod['qh'][:, h, cs], rhs=Sb[h][:], start=False, stop=True),
                     reads=['pr_qh', 'Sb%d' % h], writes=[('p_o', po)])
                S.op('pe', lambda e: e.matmul(p_ds[i2][:], lhsT=ktok[h][:], rhs=vb[:, c, h * 128:(h + 1) * 128], start=True, stop=True),
                     reads=['ktok%d' % h, 'vb'], writes=['p_ds%d' % i2])
                S.op('dve', lambda e: e.scalar_tensor_tensor(out=St[h][:], in0=St[h][:], scalar=ebl[:, h, c:c + 1], in1=p_ds[i2][:],
                                                             op0=ALU.mult, op1=ALU.add),
                     reads=['St%d' % h, 'ebl', 'p_ds%d' % i2], writes=['St%d' % h])
                S.op('act', lambda e: e.copy(out=Sb[h][:], in_=St[h][:]), reads=['St%d' % h], writes=['Sb%d' % h])
            gens = [head_gen(h) for h in range(NH)]
            alive = list(gens)
            while alive:
                nxt_alive = []
                for g_ in alive:
                    try:
                        next(g_)
                        nxt_alive.append(g_)
                    except StopIteration:
                        pass
                alive = nxt_alive
            okeys = [('p_o', po)]
            S.op('act', lambda e: e.activation(out=sq[:], in_=p_o[po][:], func=AF.Square), reads=okeys, writes=['sq'])
            S.op('dve', lambda e: e.tensor_reduce(out=ss[po][:], in_=sq[:], axis=AX.X, op=ALU.add), reads=['sq'], writes=['ss%d' % po])
            S.op('dve', lambda e: e.tensor_scalar(out=ss[po][:], in0=ss[po][:], scalar1=1.0 / 128, scalar2=1e-6, op0=ALU.mult, op1=ALU.add),
                 reads=['ss%d' % po], writes=['ss%d' % po])
            S.op('act', lambda e: e.sqrt(out=ss[po][:], in_=ss[po][:]), reads=['ss%d' % po], writes=['ss%d' % po])
            S.op('dve', lambda e: e.reciprocal(out=ss[po][:], in_=ss[po][:]), reads=['ss%d' % po], writes=['ss%d' % po])
            for h in range(NH):
                S.op('dve', lambda e, h=h: e.scalar_tensor_tensor(out=ogt[po][:, h * 128:(h + 1) * 128], in0=p_o[po][:, h, :],
                                                                 scalar=ss[po][:, h:h + 1], in1=g2[:, c, h * 128:(h + 1) * 128],
                                                                 op0=ALU.mult, op1=ALU.mult),
                     reads=[('p_o', po), 'ss%d' % po, 'g2'], writes=[('ogt', po, h)])
            S.dma('sp', og[t0 + c * 64:t0 + (c + 1) * 64, :], ogt[po][:], reads=[('ogt', po, h) for h in range(NH)], writes=['og'])
            cc += 1
    barrier(S)
    A.close()


def emit_postnorm(S, nc, *, NT, D, y, h, g, out, eps=1e-6):
    A = Alloc(nc)
    yb = [A.sb("yb%d" % i, [128, D], F32) for i in range(2)]
    hb = [A.sb("hb%d" % i, [128, D], F32) for i in range(2)]
    gb = A.sb("gb", [128, D], F32)
    junk = A.sb("junk", [128, D], BF16)
    ss = [A.sb("ss%d" % i, [128, 1], F32) for i in range(2)]
    S.dma('sp', gb[:], g[0:1, :].partition_broadcast(128), writes=['gb'])
    for j in range(NT):
        sl = j % 2
        S.dma('sp', yb[sl][:], y[j * 128:(j + 1) * 128, :], writes=['yb%d' % sl])
        S.dma('sp', hb[sl][:], h[j * 128:(j + 1) * 128, :], writes=['hb%d' % sl])
        S.op('dve', lambda e: e.memset(ss[sl][:], 0.0), writes=['ss%d' % sl])
        S.op('act', lambda e: e.activation(out=junk[:], in_=yb[sl][:], func=AF.Square, accum_out=ss[sl][:]),
             reads=['yb%d' % sl], writes=['junk', 'ss%d' % sl])
        S.op('dve', lambda e: e.tensor_scalar(out=ss[sl][:], in0=ss[sl][:], scalar1=1.0 / D, scalar2=eps,
                                              op0=ALU.mult, op1=ALU.add), reads=['ss%d' % sl], writes=['ss%d' % sl])
        S.op('act', lambda e: e.sqrt(out=ss[sl][:], in_=ss[sl][:]), reads=['ss%d' % sl], writes=['ss%d' % sl])
        S.op('dve', lambda e: e.reciprocal(out=ss[sl][:], in_=ss[sl][:]), reads=['ss%d' % sl], writes=['ss%d' % sl])
        S.op('dve', lambda e: e.scalar_tensor_tensor(out=yb[sl][:], in0=yb[sl][:], scalar=ss[sl][:, 0:1], in1=gb[:],
                                                     op0=ALU.mult, op1=ALU.mult),
             reads=['yb%d' % sl, 'ss%d' % sl, 'gb'], writes=['yb%d' % sl])
        S.op('pool', lambda e: e.tensor_tensor(out=hb[sl][:], in0=hb[sl][:], in1=yb[sl][:], op=ALU.add),
             reads=['yb%d' % sl, 'hb%d' % sl], writes=['hb%d' % sl])
        S.dma('pool', out[j * 128:(j + 1) * 128, :], hb[sl][:], reads=['hb%d' % sl], writes=['out'])
    barrier(S)
    A.close()


def emit_attn(S, nc, *, Lp, zF, zT, lamv, lambda_init, sublng, og, rope, perm, amask, padbias, NH=2, npad=112):
    NT = Lp // 128
    TB = 512
    scale = 128 ** -0.5
    A = Alloc(nc)
    f32 = lambda name, shape: A.sb(name, shape, F32)
    bf = lambda name, shape: A.sb(name, shape, BF16)
    kb = bf("kb", [128, 2, Lp])
    Vb = bf("Vb", [128, NT, 257])
    qb = [bf("qb%d" % i, [128, 2, TB]) for i in range(2)]
    xT = f32("xT", [128, 2, TB]); cs = f32("cs", [128, 2, TB]); t1 = f32("t1", [128, TB]); t2 = f32("t2", [128, TB])
    vst = f32("vst", [128, 8, 256])
    pm = f32("pm", [128, 128])
    mk_f = f32("mk_f", [128, 4, 512]); mk = bf("mk", [128, 4, 512])
    pbias = f32("pbias", [128, 1]); zbias = f32("zbias", [128, 1])
    PT = [bf("PT%d" % i, [128, 512]) for i in range(3)]
    oc = [f32("oc%d" % i, [128, 4, 256]) for i in range(2)]
    den = f32("den", [128, 8])
    gt = f32("gt", [128, 4, 256]); g2 = f32("g2", [128, 4, 256]); sgb = f32("sgb", [128, 256])
    av = f32("av", [128, 4, 256]); sq = f32("sq", [128, 4, 256]); ssq = f32("ssq", [128, 4])
    ogt = f32("ogt", [128, 4, 256])
    lv = f32("lv", [128, 4, 128]); lp_ = f32("lp_", [128, 2, 128]); ld = f32("ld", [128, 2]); nlam = f32("nlam", [128, 1])
    p_s = [A.ps("p_s%d" % i, [128, 512], F32) for i in range(2)]
    p_a = [A.ps("p_a%d" % i, [128, 512], F32) for i in range(4)]
    p_r = A.ps("p_r", [128, 512], F32)

    S.dma('sp', pm[:], perm[:, :], writes=['pm'])
    S.dma('sp', mk_f[:], amask.rearrange("i p q -> p i q"), writes=['mk_f'])
    S.op('dve', lambda e: e.tensor_copy(out=mk[:], in_=mk_f[:]), reads=['mk_f'], writes=['mk'])
    S.dma('sp', pbias[:], padbias[:, :], writes=['pbias'])
    S.op('pool', lambda e: e.memset(zbias[:], 0.0), writes=['zbias'])
    S.dma('sp', sgb[:], sublng[0:1, :].partition_broadcast(128), writes=['sgb'])
    S.op('dve', lambda e: e.tensor_scalar(out=sgb[:], in0=sgb[:], scalar1=float(1.0 - lambda_init), scalar2=None, op0=ALU.mult),
         reads=['sgb'], writes=['sgb'])
    for i in range(4):
        S.dma('sp', lv[:, i, :], lamv[i:i + 1, :].partition_broadcast(128), writes=[('lv', i)])
    S.op('dve', lambda e: e.tensor_tensor(out=lp_[:, 0, :], in0=lv[:, 0, :], in1=lv[:, 1, :], op=ALU.mult),
         reads=[('lv', 0), ('lv', 1)], writes=['lp0'])
    S.op('dve', lambda e: e.tensor_tensor(out=lp_[:, 1, :], in0=lv[:, 2, :], in1=lv[:, 3, :], op=ALU.mult),
         reads=[('lv', 2), ('lv', 3)], writes=['lp1'])
    S.op('dve', lambda e: e.tensor_reduce(out=ld[:], in_=lp_[:], axis=AX.X, op=ALU.add), reads=['lp0', 'lp1'], writes=['ld'])
    S.op('act', lambda e: e.activation(out=ld[:], in_=ld[:], func=AF.Exp), reads=['ld'], writes=['ld'])
    S.op('dve', lambda e: e.tensor_tensor(out=nlam[:], in0=ld[:, 1:2], in1=ld[:, 0:1], op=ALU.subtract), reads=['ld'], writes=['nlam'])
    S.op('dve', lambda e: e.tensor_scalar(out=nlam[:], in0=nlam[:], scalar1=float(-lambda_init), scalar2=None, op0=ALU.add),
         reads=['nlam'], writes=['nlam'])

    nblk = (Lp + TB - 1) // TB

    def rope_block(row0, t0, n, dst, dkey):
        S.dma('sp', xT[:, :, 0:n], zF[row0:row0 + 256, t0:t0 + n].rearrange("(c p) t -> p c t", p=128), writes=['xT'])
        S.dma('sp', cs[:, :, 0:n], rope[:, :, t0:t0 + n].rearrange("a p t -> p a t"), writes=['cs'])
        for c in range(2):
            S.op('pe', lambda e: e.matmul(p_r[:, 0:n], lhsT=pm[:], rhs=xT[:, c, 0:n], start=True, stop=True),
                 reads=['pm', 'xT'], writes=['p_r'])
            S.op('dve', lambda e: e.tensor_tensor(out=t1[:, 0:n], in0=xT[:, c, 0:n], in1=cs[:, 0, 0:n], op=ALU.mult),
                 reads=['xT', 'cs'], writes=['t1'])
            S.op('dve', lambda e: e.tensor_tensor(out=t2[:, 0:n], in0=p_r[:, 0:n], in1=cs[:, 1, 0:n], op=ALU.mult),
                 reads=['p_r', 'cs'], writes=['t2'])
            S.op('pool', lambda e: e.tensor_tensor(out=dst(c), in0=t1[:, 0:n], in1=t2[:, 0:n], op=ALU.add),
                 reads=['t1', 't2'], writes=[dkey])

    pc = 0
    for hd in range(NH):
        for b in range(nblk):
            t0 = b * TB
            n = min(TB, Lp - t0)
            rope_block(NH * 256 + hd * 256, t0, n, lambda c: kb[:, c, t0:t0 + n], 'kb')
        for j0 in range(0, NT, 8):
            nj = min(8, NT - j0)
            S.dma('sp', vst[:, 0:nj, :], zT[j0 * 128:(j0 + nj) * 128, hd * 256:(hd + 1) * 256].rearrange("(j p) e -> p j e", p=128),
                  writes=['vst'])
            S.op('act', lambda e: e.copy(out=Vb[:, j0:j0 + nj, 0:256], in_=vst[:, 0:nj, :]), reads=['vst'], writes=['Vb'])
        S.op('pool', lambda e: e.memset(Vb[:, :, 256:257], 1.0), writes=['Vb'])
        for b in range(nblk):
            t0 = b * TB
            n = min(TB, Lp - t0)
            nq = n // 128
            qs = b % 2
            rope_block(hd * 256, t0, n, lambda c: qb[qs][:, c, 0:n], 'qb%d' % qs)
            S.dma('sp', gt[:, 0:nq, :], zT[t0:t0 + n, NH * 256 + hd * 256:NH * 256 + (hd + 1) * 256].rearrange("(j p) e -> p j e", p=128),
                  writes=['gt'])
            S.op('act', lambda e: e.activation(out=g2[:, 0:nq, :], in_=gt[:, 0:nq, :], func=AF.Silu), reads=['gt'], writes=['g2'])
            S.op('dve', lambda e: e.tensor_tensor(out=g2[:, 0:nq, :], in0=g2[:, 0:nq, :],
                                                  in1=sgb[:].unsqueeze(1).to_broadcast([128, nq, 256]), op=ALU.mult),
                 reads=['g2', 'sgb'], writes=['g2'])
            nkt = 4 * b + nq
            for c in range(2):
                idx = []
                for kt in range(nkt):
                    idx.append((pc % 2, pc % 3))
                    pc += 1

                def emit_s(kt):
                    ps = idx[kt][0]
                    S.op('pe', lambda e: e.matmul(p_s[ps][:, 0:n], lhsT=kb[:, c, kt * 128:(kt + 1) * 128], rhs=qb[qs][:, c, 0:n],
                                                  start=True, stop=True),
                         reads=['kb', 'qb%d' % qs], writes=['p_s%d' % ps])
                emit_s(0)
                for kt in range(nkt):
                    ps, pt = idx[kt]
                    if kt + 1 < nkt:
                        emit_s(kt + 1)
                    S.op('act', lambda e: e.activation(out=PT[pt][:, 0:n], in_=p_s[ps][:, 0:n], func=AF.Exp, scale=float(scale),
                                                       bias=(pbias[:, 0:1] if kt == 0 else zbias[:, 0:1])),
                         reads=['p_s%d' % ps, 'pbias', 'zbias'], writes=['PT%d' % pt])
                    i = kt - 4 * b
                    if i >= 0:
                        S.op('dve', lambda e: e.tensor_tensor(out=PT[pt][:, 0:n], in0=PT[pt][:, 0:n], in1=mk[:, i, 0:n], op=ALU.mult),
                             reads=['PT%d' % pt, 'mk'], writes=['PT%d' % pt])
                    for j in range(nq):
                        if i > j:
                            continue
                        last = (kt == 4 * b + j)
                        S.op('pe', lambda e: e.matmul(p_a[j][:, 0:257], lhsT=PT[pt][:, j * 128:(j + 1) * 128], rhs=Vb[:, kt, :],
                                                      start=(kt == 0), stop=last),
                             reads=['PT%d' % pt, 'Vb'], writes=['p_a%d' % j])
                for j in range(nq):
                    S.op('dve', lambda e: e.tensor_scalar(out=den[:, c * 4 + j:c * 4 + j + 1], in0=p_a[j][:, 256:257], scalar1=1e-30, scalar2=None,
                                                          op0=ALU.add), reads=['p_a%d' % j], writes=[('den', c, j)])
                    S.op('dve', lambda e: e.reciprocal(out=den[:, c * 4 + j:c * 4 + j + 1], in_=den[:, c * 4 + j:c * 4 + j + 1]),
                         reads=[('den', c, j)], writes=[('den', c, j)])
                    S.op('act', lambda e: e.activation(out=oc[c][:, j, :], in_=p_a[j][:, 0:256], func=AF.Copy,
                                                       scale=den[:, c * 4 + j:c * 4 + j + 1]),
                         reads=['p_a%d' % j, ('den', c, j)], writes=[('oc', c, j)])
            ock = [('oc', c, j) for c in range(2) for j in range(nq)]
            S.op('dve', lambda e: e.scalar_tensor_tensor(out=av[:, 0:nq, :], in0=oc[1][:, 0:nq, :], scalar=nlam[:, 0:1], in1=oc[0][:, 0:nq, :],
                                                         op0=ALU.mult, op1=ALU.add), reads=ock + ['nlam'], writes=['av'])
            S.op('act', lambda e: e.activation(out=sq[:, 0:nq, :], in_=av[:, 0:nq, :], func=AF.Square), reads=['av'], writes=['sq'])
            S.op('dve', lambda e: e.tensor_reduce(out=ssq[:, 0:nq], in_=sq[:, 0:nq, :], axis=AX.X, op=ALU.add), reads=['sq'], writes=['ssq'])
            S.op('dve', lambda e: e.tensor_scalar(out=ssq[:, 0:nq], in0=ssq[:, 0:nq], scalar1=1.0 / 256, scalar2=1e-5, op0=ALU.mult, op1=ALU.add),
                 reads=['ssq'], writes=['ssq'])
            S.op('act', lambda e: e.sqrt(out=ssq[:, 0:nq], in_=ssq[:, 0:nq]), reads=['ssq'], writes=['ssq'])
            S.op('dve', lambda e: e.reciprocal(out=ssq[:, 0:nq], in_=ssq[:, 0:nq]), reads=['ssq'], writes=['ssq'])
            for j in range(nq):
                S.op('dve', lambda e, j=j: e.scalar_tensor_tensor(out=ogt[:, j, :], in0=av[:, j, :], scalar=ssq[:, j:j + 1], in1=g2[:, j, :],
                                                                 op0=ALU.mult, op1=ALU.mult),
                     reads=['av', 'ssq', 'g2'], writes=['ogt'])
            S.dma('sp', og[t0:t0 + n, hd * 256:(hd + 1) * 256].rearrange("(j p) e -> p j e", p=128), ogt[:, 0:nq, :],
                  reads=['ogt'], writes=['og'])
    barrier(S)
    A.close()


def emit_rwkv(S, nc, *, Lp, zF, zT, w2, a2, vecs, lnx, og, cmat, TB=512):
    NP = 4
    NCH = TB // 64
    A = Alloc(nc)
    f32 = lambda name, shape: A.sb(name, shape, F32)
    bf = lambda name, shape: A.sb(name, shape, BF16)
    cm = f32("cm", [128, 6, 128])
    idb = bf("idb", [128, 128]); onesb = bf("onesb", [128, 1])
    w2f = f32("w2f", [128, 512]); a2f = f32("a2f", [128, 512]); w2b = bf("w2b", [128, 512]); a2b = bf("a2b", [128, 512])
    vc = f32("vc", [128, 5, 4]); omka = f32("omka", [128, 4])
    lnr = f32("lnr", [128, 2, 512])
    zw = f32("zw", [128, 2, TB]); zwb = bf("zwb", [128, 2, TB])
    rT = f32("rT", [128, TB]); kT = f32("kT", [128, TB])
    lw = f32("lw", [128, TB]); G = f32("G", [128, TB]); Gx = f32("Gx", [128, TB])
    eG = f32("eG", [128, TB]); eGx = f32("eGx", [128, TB]); enG = f32("enG", [128, TB])
    av = f32("av", [128, TB]); kkv = f32("kkv", [128, TB]); tmp = f32("tmp", [128, TB]); inv = f32("inv", [128, TB])
    kp = f32("kp", [128, TB]); bet = f32("bet", [128, TB])
    smask = f32("smask", [128, TB])
    egc = f32("egc", [128, NP, NCH])
    BD = {nm: bf("bd_" + nm, [128, NP, NCH, 128]) for nm in ('KT', 'RT', 'BT', 'KK', 'RKR')}
    Vst = f32("Vst", [128, NCH, NP, 128]); Gst = f32("Gst", [128, NCH, NP, 128]); Vbd = bf("Vbd", [128, NCH, NP, 128])
    OG = f32("OG", [128, NCH, NP, 128])
    Zf = [f32("Zf%d" % p, [128, 128]) for p in range(NP)]
    Zs = [f32("Zs%d" % p, [128, 128]) for p in range(NP)]
    Zb = [bf("Zb%d" % p, [128, 128]) for p in range(NP)]
    NR = 8
    Qb = [bf("Qb%d" % r, [128, 4, 128]) for r in range(NR)]
    XYb = [[bf("XYb%d_%d" % (r, i), [128, 2, 128]) for i in range(5)] for r in range(NR)]
    Xb = [[(Qb[r][:, 0, :] if i == 0 else XYb[r][i][:, 0, :]) for i in range(5)] for r in range(NR)]
    Yb = [[(Qb[r][:, 1, :] if i == 0 else XYb[r][i][:, 1, :]) for i in range(5)] for r in range(NR)]
    cmq = bf("cmq", [128, 4, 128])
    IX = [[bf("IX%d_%d" % (r, i), [128, 128]) for i in range(6)] for r in range(NR)]
    AakT = [Qb[r][:, 2, :] for r in range(NR)]
    BrbT = [Qb[r][:, 3, :] for r in range(NR)]
    BrkT = [bf("BrkT%d" % r, [128, 128]) for r in range(NR)]
    Bb = [[bf("Bb%d_%d" % (r, i), [128, 128]) for i in range(2)] for r in range(NR)]
    BTt = [bf("BTt%d" % r, [128, 128]) for r in range(NR)]
    KKt = [bf("KKt%d" % r, [128, 128]) for r in range(NR)]
    sqb = [f32("sqb%d" % r, [128, 128]) for r in range(4)]; yb = [f32("yb%d" % r, [128, 128]) for r in range(NR)]
    s1 = f32("s1", [128, NP]); s2 = f32("s2", [128, NP]); mean = f32("mean", [128, NP]); rstd = f32("rstd", [128, NP])
    rkc = f32("rkc", [128, NP])
    ps = [A.ps("ps%d" % i, [128, 4, 128], F32) for i in range(5)]
    po = A.ps("po", [128, 4, 128], F32)
    pst = [A.ps("pst%d" % i, [128, 4, 128], BF16) for i in range(1)]
    pbig = A.ps("pbig", [128, 512], F32)
    slot_ctr = [0]

    def slot(k=1):
        b_ = slot_ctr[0] % 5
        slot_ctr[0] += 1
        if k == 1:
            return ps[b_][:, 0, :], ('p_sb', b_)
        return ps[b_][:, 0:k, :], ('p_sb', b_)

    S.dma('sp', cm[:], cmat.rearrange("i p q -> p i q"), writes=['cm'])
    S.op('dve', lambda e: e.tensor_copy(out=idb[:], in_=cm[:, 0, :]), reads=['cm'], writes=['idb'])
    S.op('pool', lambda e: e.memset(onesb[:], 1.0), writes=['onesb'])
    for qi, mi in enumerate((2, 4, 2, 3)):
        S.op('dve', lambda e, qi=qi, mi=mi: e.tensor_copy(out=cmq[:, qi, :], in_=cm[:, mi, :]), reads=['cm'], writes=['cmq'])
    S.dma('sp', w2f[:], w2[:, :], writes=['w2f'])
    S.dma('sp', a2f[:], a2[:, :], writes=['a2f'])
    S.op('dve', lambda e: e.tensor_copy(out=w2b[:], in_=w2f[:]), reads=['w2f'], writes=['w2b'])
    S.op('dve', lambda e: e.tensor_copy(out=a2b[:], in_=a2f[:]), reads=['a2f'], writes=['a2b'])
    S.dma('sp', vc[:], vecs[:, :, :], writes=['vc'])
    S.op('dve', lambda e: e.tensor_scalar(out=omka[:], in0=vc[:, 3, :], scalar1=-1.0, scalar2=1.0, op0=ALU.mult, op1=ALU.add),
         reads=['vc'], writes=['omka'])
    for i in range(2):
        S.dma('sp', lnr[:, i, :], lnx[i:i + 1, :].partition_broadcast(128), writes=[('lnr', i)])
    lnk = [('lnr', 0), ('lnr', 1)]
    S.op('pool', lambda e: e.memset(smask[:], 1.0), writes=['smask'])
    S.op('pool', lambda e: e.memset(smask[:].rearrange("p (c w) -> p c w", w=64)[:, :, 0:1], 0.0), writes=['smask'])
    for nm in BD:
        S.op('pool', lambda e, nm=nm: e.memset(BD[nm][:], 0.0), writes=[('bd', nm, p) for p in range(NP)])
    S.op('pool', lambda e: e.memset(Vst[:], 0.0), writes=[('Vst', p, hf) for p in range(NP) for hf in range(2)])
    S.op('pool', lambda e: e.memset(Gst[:], 0.0), writes=[('Gst', p, hf) for p in range(NP) for hf in range(2)])
    for p in range(NP):
        S.op('dve', lambda e, p=p: e.memset(Zf[p][:], 0.0), writes=['Zf%d' % p])
        S.op('dve', lambda e, p=p: e.memset(Zb[p][:], 0.0), writes=['Zb%d' % p])

    nblk = (Lp + TB - 1) // TB
    rr = 0
    for b in range(nblk):
        t0 = b * TB
        n = min(TB, Lp - t0)
        nch = n // 64
        for (dst, c0, key) in ((Vst, 0, 'Vst'), (Gst, 512, 'Gst')):
            src = zT[t0:t0 + n, c0:c0 + 512].rearrange("(c s) (p two n) -> s c p two n", s=64, two=2, n=64)
            for p in range(NP):
                S.dma('sp', dst[0:64, 0:nch, p, 0:64], src[:, :, p, 0, :], writes=[(key, p, 0)])
                S.dma('sp', dst[64:128, 0:nch, p, 64:128], src[:, :, p, 1, :], writes=[(key, p, 1)])
        vk = [('Vst', p, hf) for p in range(NP) for hf in range(2)]
        gk = [('Gst', p, hf) for p in range(NP) for hf in range(2)]
        S.op('act', lambda e: e.copy(out=Vbd[:, 0:nch], in_=Vst[:, 0:nch]), reads=vk, writes=['Vbd'])
        S.op('act', lambda e: e.activation(out=Gst[:, 0:nch], in_=Gst[:, 0:nch], func=AF.Silu), reads=gk, writes=gk)
        S.dma('sp', zw[:, :, 0:n], zF[1024:1280, t0:t0 + n].rearrange("(a p) t -> p a t", p=128), writes=['zw'])
        S.op('act', lambda e: e.activation(out=zwb[:, 0, 0:n], in_=zw[:, 0, 0:n], func=AF.Tanh), reads=['zw'], writes=['zwb0'])
        S.op('dve', lambda e: e.tensor_copy(out=zwb[:, 1, 0:n], in_=zw[:, 1, 0:n]), reads=['zw'], writes=['zwb1'])
        for p in range(NP):
            S.dma('sp', rT[:, 0:n], zF[p * 128:(p + 1) * 128, t0:t0 + n], writes=['rT'])
            S.dma('sp', kT[:, 0:n], zF[512 + p * 128:512 + (p + 1) * 128, t0:t0 + n], writes=['kT'])
            S.op('pe', lambda e: e.matmul(pbig[:, 0:n], lhsT=w2b[:, p * 128:(p + 1) * 128], rhs=zwb[:, 0, 0:n], start=True, stop=True),
                 reads=['w2b', 'zwb0'], writes=['p_big'])
            S.op('act', lambda e: e.activation(out=lw[:, 0:n], in_=pbig[:, 0:n], func=AF.Sigmoid, bias=vc[:, 0, p:p + 1]),
                 reads=['p_big', 'vc'], writes=['lw'])
            S.op('pe', lambda e: e.matmul(pbig[:, 0:n], lhsT=a2b[:, p * 128:(p + 1) * 128], rhs=zwb[:, 1, 0:n], start=True, stop=True),
                 reads=['a2b', 'zwb1', 'lw'], writes=['p_big'])
            S.op('act', lambda e: e.activation(out=av[:, 0:n], in_=pbig[:, 0:n], func=AF.Sigmoid, bias=vc[:, 1, p:p + 1]),
                 reads=['p_big', 'vc'], writes=['av'])
            S.op('dve', lambda e: e.tensor_scalar(out=lw[:, 0:n], in0=lw[:, 0:n], scalar1=-0.606531, scalar2=None, op0=ALU.mult),
                 reads=['lw'], writes=['lw'])
            S.op('dve', lambda e: e.tensor_tensor_scan(out=G[:, 0:n], data0=smask[:, 0:n], data1=lw[:, 0:n], initial=0.0,
                                                       op0=ALU.mult, op1=ALU.add), reads=['lw', 'smask'], writes=['G'])
            S.op('pool', lambda e: e.tensor_tensor(out=Gx[:, 0:n], in0=G[:, 0:n], in1=lw[:, 0:n], op=ALU.subtract),
                 reads=['G', 'lw'], writes=['Gx'])
            S.op('act', lambda e: e.activation(out=eG[:, 0:n], in_=G[:, 0:n], func=AF.Exp), reads=['G'], writes=['eG'])
            S.op('act', lambda e: e.activation(out=enG[:, 0:n], in_=G[:, 0:n], func=AF.Exp, scale=-1.0), reads=['G'], writes=['enG'])
            S.op('act', lambda e: e.activation(out=eGx[:, 0:n], in_=Gx[:, 0:n], func=AF.Exp), reads=['Gx'], writes=['eGx'])
            S.op('act', lambda e: e.copy(out=egc[:, p, 0:nch], in_=eG[:, 0:n].rearrange("p (c w) -> p c w", w=64)[:, :, 63]),
                 reads=['eG'], writes=['egc'])
            S.op('dve', lambda e: e.tensor_scalar(out=kkv[:, 0:n], in0=kT[:, 0:n], scalar1=vc[:, 2, p:p + 1], scalar2=None, op0=ALU.mult),
                 reads=['kT', 'vc'], writes=['kkv'])
            S.op('pool', lambda e: e.tensor_tensor(out=tmp[:, 0:n], in0=kkv[:, 0:n], in1=kkv[:, 0:n], op=ALU.mult),
                 reads=['kkv'], writes=['tmp'])
            S.op('pe', lambda e: e.matmul(pbig[:, 0:n], lhsT=cm[:, 1, :], rhs=tmp[:, 0:n], start=True, stop=True),
                 reads=['cm', 'tmp', 'av'], writes=['p_big'])
            S.op('act', lambda e: e.sqrt(out=inv[:, 0:n], in_=pbig[:, 0:n]), reads=['p_big'], writes=['inv'])
            S.op('dve', lambda e: e.tensor_scalar(out=inv[:, 0:n], in0=inv[:, 0:n], scalar1=1e-12, scalar2=None, op0=ALU.max),
                 reads=['inv'], writes=['inv'])
            S.op('dve', lambda e: e.reciprocal(out=inv[:, 0:n], in_=inv[:, 0:n]), reads=['inv'], writes=['inv'])
            S.op('dve', lambda e: e.tensor_tensor(out=kkv[:, 0:n], in0=kkv[:, 0:n], in1=inv[:, 0:n], op=ALU.mult),
                 reads=['kkv', 'inv'], writes=['kkv'])
            S.op('dve', lambda e: e.tensor_scalar(out=kp[:, 0:n], in0=av[:, 0:n], scalar1=vc[:, 3, p:p + 1], scalar2=omka[:, p:p + 1],
                                                  op0=ALU.mult, op1=ALU.add), reads=['av', 'vc', 'omka'], writes=['kp'])
            S.op('pool', lambda e: e.tensor_tensor(out=kp[:, 0:n], in0=kp[:, 0:n], in1=kT[:, 0:n], op=ALU.mult),
                 reads=['kp', 'kT'], writes=['kp'])
            S.op('dve', lambda e: e.scalar_tensor_tensor(out=bet[:, 0:n], in0=kkv[:, 0:n], scalar=-1.0, in1=av[:, 0:n],
                                                         op0=ALU.mult, op1=ALU.mult), reads=['kkv', 'av'], writes=['bet'])
            S.op('dve', lambda e: e.scalar_tensor_tensor(out=tmp[:, 0:n], in0=rT[:, 0:n], scalar=vc[:, 4, p:p + 1], in1=kp[:, 0:n],
                                                         op0=ALU.mult, op1=ALU.mult), reads=['rT', 'vc', 'kp', 'tmp'], writes=['tmp'])

            def bdw(nm, a_, b_, eng):
                for hf in range(2):
                    pr = slice(hf * 64, (hf + 1) * 64)
                    dst = BD[nm][pr, p, 0:nch, hf * 64:(hf + 1) * 64]
                    va = a_[pr, 0:n].rearrange("p (c w) -> p c w", w=64)
                    if b_ is None:
                        S.op(eng, lambda e: e.tensor_copy(out=dst, in_=va), reads=['tmp'], writes=[('bd', nm, p)])
                    else:
                        vb_ = b_[pr, 0:n].rearrange("p (c w) -> p c w", w=64)
                        S.op(eng, lambda e: e.tensor_tensor(out=dst, in0=va, in1=vb_, op=ALU.mult),
                             reads=['kkv', 'eGx', 'rT', 'eG', 'bet', 'enG', 'kp'], writes=[('bd', nm, p)])
            bdw('KT', kkv, eGx, 'dve')
            bdw('RT', rT, eG, 'pool')
            bdw('BT', bet, enG, 'dve')
            bdw('KK', kp, enG, 'pool')
            bdw('RKR', tmp, None, 'dve')
        def indep_gen(p, c):
            r = (c % 2) * NP + p
            KT = BD['KT'][:, p, c, :]; RT = BD['RT'][:, p, c, :]; BT = BD['BT'][:, p, c, :]; KK = BD['KK'][:, p, c, :]
            bk = lambda nm: [('bd', nm, p)]
            col = (c % 2) * NP + p
            S.op('pe', lambda e: e.matmul(pbig[:, col:col + 1], lhsT=BD['RKR'][:, p, c, :], rhs=onesb[:], start=True, stop=True),
                 reads=bk('RKR') + ['onesb'], writes=['p_big'])
            q4, q4k = slot(4); rkk_, rkkk = slot()
            S.op('pe', lambda e: e.matmul(q4[:, 0, :], lhsT=BT, rhs=KT, start=True, stop=True), reads=bk('BT') + bk('KT'), writes=[q4k])
            S.op('pe', lambda e: e.matmul(q4[:, 1, :], lhsT=KT, rhs=BT, start=True, stop=True), reads=bk('BT') + bk('KT'), writes=[q4k])
            S.op('pe', lambda e: e.matmul(q4[:, 2, :], lhsT=KK, rhs=KT, start=True, stop=True), reads=bk('KK') + bk('KT'), writes=[q4k])
            S.op('pe', lambda e: e.matmul(q4[:, 3, :], lhsT=BT, rhs=RT, start=True, stop=True), reads=bk('BT') + bk('RT'), writes=[q4k])
            S.op('pe', lambda e: e.matmul(rkk_, lhsT=KK, rhs=RT, start=True, stop=True), reads=bk('KK') + bk('RT'), writes=[rkkk])
            S.op('dve', lambda e: e.tensor_tensor(out=Qb[r][:], in0=q4, in1=cmq[:], op=ALU.mult), reads=[q4k, 'cmq'],
                 writes=[('X', r, 0), ('Y', r, 0), ('Aak', r), ('Brb', r)])
            S.op('dve', lambda e: e.tensor_tensor(out=BrkT[r][:], in0=rkk_, in1=cm[:, 3, :], op=ALU.mult), reads=[rkkk, 'cm'], writes=[('Brk', r)])
            S.op('pool', lambda e: e.tensor_tensor(out=IX[r][0][:], in0=Xb[r][0], in1=idb[:], op=ALU.add),
                 reads=[('X', r, 0), 'idb'], writes=[('IX', r, 0)])
            yield
            for i in range(5):
                if i < 4:
                    xy, xyk = slot(2)
                    S.op('pe', lambda e: e.matmul(xy[:, 0, :], lhsT=Yb[r][i], rhs=Xb[r][i], start=True, stop=True),
                         reads=[('X', r, i), ('Y', r, i)], writes=[xyk])
                    S.op('pe', lambda e: e.matmul(xy[:, 1, :], lhsT=Xb[r][i], rhs=Yb[r][i], start=True, stop=True),
                         reads=[('X', r, i), ('Y', r, i)], writes=[xyk])
                    if (i + p) % 2 == 0:
                        S.op('act', lambda e: e.copy(out=XYb[r][i + 1][:], in_=xy), reads=[xyk], writes=[('X', r, i + 1), ('Y', r, i + 1)])
                    else:
                        S.op('dve', lambda e: e.tensor_copy(out=XYb[r][i + 1][:], in_=xy), reads=[xyk], writes=[('X', r, i + 1), ('Y', r, i + 1)])
                    S.op('pool', lambda e: e.tensor_tensor(out=IX[r][i + 1][:], in0=Xb[r][i + 1], in1=idb[:], op=ALU.add),
                         reads=[('X', r, i + 1), 'idb'], writes=[('IX', r, i + 1)])
                else:
                    xs, xsk = slot()
                    S.op('pe', lambda e: e.matmul(xs, lhsT=Yb[r][i], rhs=Xb[r][i], start=True, stop=True),
                         reads=[('X', r, i), ('Y', r, i)], writes=[xsk])
                    S.op('dve', lambda e: e.tensor_tensor(out=IX[r][5][:], in0=xs, in1=idb[:], op=ALU.add),
                         reads=[xsk, 'idb'], writes=[('IX', r, 5)])
                yield
            S.op('pe', lambda e: e.transpose(out=pst[0][:, 0, :], in_=BT, identity=idb[:]), reads=bk('BT') + ['idb'], writes=['p_st'])
            S.op('pe', lambda e: e.transpose(out=pst[0][:, 1, :], in_=KK, identity=idb[:]), reads=bk('KK') + ['idb'], writes=['p_st'])
            S.op('act', lambda e: e.copy(out=BTt[r][:], in_=pst[0][:, 0, :]), reads=['p_st'], writes=[('BTt', r)])
            S.op('dve', lambda e: e.tensor_copy(out=KKt[r][:], in_=pst[0][:, 1, :]), reads=['p_st'], writes=[('KKt', r)])

        def dep_gen(p, c, okeys, oslots):
            r = (c % 2) * NP + p
            KT = BD['KT'][:, p, c, :]; RT = BD['RT'][:, p, c, :]
            bk = lambda nm: [('bd', nm, p)]
            b0, b0k = slot()
            S.op('pe', lambda e: e.matmul(b0, lhsT=KT, rhs=Zb[p][:], start=True, stop=False), reads=bk('KT') + ['Zb%d' % p], writes=[b0k])
            S.op('pe', lambda e: e.matmul(b0, lhsT=AakT[r], rhs=Vbd[:, c, p, :], start=False, stop=True),
                 reads=[('Aak', r), 'Vbd'], writes=[b0k])
            S.op('act', lambda e: e.copy(out=Bb[p][0][:], in_=b0), reads=[b0k], writes=[('B', p, 0)])
            yield
            cur = 0
            for i in range(6):
                bs, bsk = slot()
                S.op('pe', lambda e: e.matmul(bs, lhsT=IX[r][i][:], rhs=Bb[p][cur][:], start=True, stop=True),
                     reads=[('IX', r, i), ('B', p, cur)], writes=[bsk])
                nxt = 1 - cur
                if i % 2 == 0:
                    S.op('act', lambda e: e.copy(out=Bb[p][nxt][:], in_=bs), reads=[bsk], writes=[('B', p, nxt)])
                else:
                    S.op('dve', lambda e: e.tensor_copy(out=Bb[p][nxt][:], in_=bs), reads=[bsk], writes=[('B', p, nxt)])
                cur = nxt
                yield
            U = Bb[p][cur]
            Uk = ('B', p, cur)
            o_ps, o_k = po[:, p, :], 'p_o'
            S.op('pe', lambda e: e.matmul(o_ps, lhsT=RT, rhs=Zb[p][:], start=True, stop=False), reads=bk('RT') + ['Zb%d' % p], writes=[o_k])
            S.op('pe', lambda e: e.matmul(o_ps, lhsT=BrbT[r], rhs=U[:], start=False, stop=False), reads=[('Brb', r), Uk], writes=[o_k])
            S.op('pe', lambda e: e.matmul(o_ps, lhsT=BrkT[r][:], rhs=Vbd[:, c, p, :], start=False, stop=True), reads=[('Brk', r), 'Vbd'], writes=[o_k])
            okeys[p] = o_k
            oslots[p] = o_ps
            d_ps, d_k = slot()
            S.op('pe', lambda e: e.matmul(d_ps, lhsT=BTt[r][:], rhs=U[:], start=True, stop=False), reads=[('BTt', r), Uk], writes=[d_k])
            S.op('pe', lambda e: e.matmul(d_ps, lhsT=KKt[r][:], rhs=Vbd[:, c, p, :], start=False, stop=True), reads=[('KKt', r), 'Vbd'], writes=[d_k])
            S.op('act', lambda e: e.activation(out=Zs[p][:], in_=Zf[p][:], func=AF.Copy, scale=egc[:, p, c:c + 1]),
                 reads=['Zf%d' % p, 'egc'], writes=['Zs%d' % p])
            S.op('dve', lambda e: e.scalar_tensor_tensor(out=Zf[p][:], in0=d_ps, scalar=egc[:, p, c:c + 1], in1=Zs[p][:],
                                                         op0=ALU.mult, op1=ALU.add), reads=[d_k, 'egc', 'Zs%d' % p], writes=['Zf%d' % p])
            S.op('act', lambda e: e.copy(out=Zb[p][:], in_=Zf[p][:]), reads=['Zf%d' % p], writes=['Zb%d' % p])
            yield
            S.op('dve', lambda e: e.tensor_reduce(out=s1[:, p:p + 1], in_=o_ps, axis=AX.X, op=ALU.add), reads=[o_k], writes=[('s1', p)])
            S.op('act', lambda e: e.activation(out=sqb[p][:], in_=o_ps, func=AF.Square), reads=[o_k], writes=[('sqb', p)])
            S.op('dve', lambda e: e.tensor_reduce(out=s2[:, p:p + 1], in_=sqb[p][:], axis=AX.X, op=ALU.add), reads=[('sqb', p)], writes=[('s2', p)])

        def run_rr(gens):
            alive = list(gens)
            while alive:
                nxt_alive = []
                for g_ in alive:
                    try:
                        next(g_)
                        nxt_alive.append(g_)
                    except StopIteration:
                        pass
                alive = nxt_alive

        run_rr([indep_gen(p, 0) for p in range(NP)])
        for c in range(nch):
            okeys = [None] * NP
            oslots = [None] * NP
            rk_ps, rk_key = pbig[:, (c % 2) * NP:(c % 2) * NP + NP], 'p_big'
            gens = []
            for p in range(NP):
                gens.append(dep_gen(p, c, okeys, oslots))
                if c + 1 < nch:
                    gens.append(indep_gen(p, c + 1))
            run_rr(gens)
            sk1 = [('s1', p) for p in range(NP)]; sk2 = [('s2', p) for p in range(NP)]
            S.op('act', lambda e: e.copy(out=rkc[:], in_=rk_ps), reads=[rk_key], writes=['rkc'])
            S.op('dve', lambda e: e.tensor_scalar(out=mean[:], in0=s1[:], scalar1=1.0 / 64, scalar2=None, op0=ALU.mult), reads=sk1, writes=['mean'])
            S.op('dve', lambda e: e.tensor_tensor(out=rstd[:], in0=mean[:], in1=mean[:], op=ALU.mult), reads=['mean'], writes=['rstd'])
            S.op('dve', lambda e: e.scalar_tensor_tensor(out=rstd[:], in0=s2[:], scalar=1.0 / 64, in1=rstd[:], op0=ALU.mult, op1=ALU.subtract),
                 reads=sk2 + ['rstd'], writes=['rstd'])
            S.op('dve', lambda e: e.tensor_scalar(out=rstd[:], in0=rstd[:], scalar1=64e-5, scalar2=None, op0=ALU.add), reads=['rstd'], writes=['rstd'])
            S.op('act', lambda e: e.sqrt(out=rstd[:], in_=rstd[:]), reads=['rstd'], writes=['rstd'])
            S.op('dve', lambda e: e.reciprocal(out=rstd[:], in_=rstd[:]), reads=['rstd'], writes=['rstd'])
            for p in range(NP):
                r = p % NR
                S.op('dve', lambda e: e.tensor_scalar(out=yb[r][:], in0=oslots[p], scalar1=mean[:, p:p + 1], scalar2=rstd[:, p:p + 1],
                                                      op0=ALU.subtract, op1=ALU.mult), reads=[okeys[p], 'mean', 'rstd'], writes=['yb%d' % r])
                S.op('pool', lambda e: e.tensor_tensor(out=yb[r][:], in0=yb[r][:], in1=lnr[:, 0, p * 128:(p + 1) * 128], op=ALU.mult),
                     reads=['yb%d' % r] + lnk, writes=['yb%d' % r])
                S.op('pool', lambda e: e.tensor_tensor(out=yb[r][:], in0=yb[r][:], in1=lnr[:, 1, p * 128:(p + 1) * 128], op=ALU.add),
                     reads=['yb%d' % r] + lnk, writes=['yb%d' % r])
                S.op('dve', lambda e: e.scalar_tensor_tensor(out=yb[r][:], in0=Vst[:, c, p, :], scalar=rkc[:, p:p + 1], in1=yb[r][:],
                                                             op0=ALU.mult, op1=ALU.add), reads=[('Vst', p, 0), ('Vst', p, 1), 'rkc', 'yb%d' % r], writes=['yb%d' % r])
                S.op('pool', lambda e: e.tensor_tensor(out=OG[:, c, p, :], in0=yb[r][:], in1=Gst[:, c, p, :], op=ALU.mult),
                     reads=['yb%d' % r, ('Gst', p, 0), ('Gst', p, 1)], writes=[('OG', c, p)])
        dst = og[t0:t0 + n, 0:512].rearrange("(c s) (p two n) -> s c p two n", s=64, two=2, n=64)
        ogk = [('OG', c, p) for c in range(nch) for p in range(NP)]
        for p in range(NP):
            S.dma('sp', dst[:, :, p, 0, :], OG[0:64, 0:nch, p, 0:64], reads=ogk, writes=['og'])
            S.dma('sp', dst[:, :, p, 1, :], OG[64:128, 0:nch, p, 64:128], reads=ogk, writes=['og'])
    barrier(S)
    A.close()


def sched_allgather(S, src, dst, reads=(), writes=(), ncores=8):
    q = 'pool'
    reads, writes = S._split(reads, writes)
    for t in S._deps(reads, writes):
        S._wait(q, t)
    s = S.drr
    S.drr = (S.drr + 1) % len(S.dsem)
    if S.dcnt[s] > 0:
        S._wait(q, (S.dsem[s], 16 * S.dcnt[s]))
    inst = S.nc.gpsimd.collective_compute("AllGather", mybir.AluOpType.bypass, replica_groups=[list(range(ncores))],
                                          ins=[src], outs=[dst])
    S.dcnt[s] += 1
    inst.then_inc(S.dsem[s], 16)
    tok = (S.dsem[s], 16 * S.dcnt[s])
    S._update(tok, reads, writes)
    return tok


D_MODEL = 4096
N_META = 16
SEQ = 16384
NPAD = 112
LP = NPAD + N_META + SEQ
NCORES = 8
NTB = 17


def _consts_common():
    return {"ident": np.eye(128, dtype=np.float32)}


def _hgrn_masks():
    m = np.zeros((3, 128, 512), np.float32)
    m[0] = 1.0
    m[0][:, ::64] = 0.0
    s = np.arange(64)[:, None]
    t = np.arange(64)[None, :]
    m[1][:64, :64] = ((s // 32 == t // 32) & (s <= t))
    m[2][:64, :64] = ((s < 32) & (t >= 32))
    return m


def _attn_consts(Lp):
    inv_freq = (500000.0 ** (-np.arange(0, 32, 2, dtype=np.float32) / 32)).astype(np.float32)
    pos = (np.arange(Lp) - NPAD).astype(np.float32)
    ang = pos[:, None] * inv_freq[None, :]
    rope = np.zeros((2, 128, Lp), np.float32)
    rope[0] = 1.0
    rope[0, 0:16] = np.cos(ang).T
    rope[0, 16:32] = np.cos(ang).T
    rope[1, 0:16] = -np.sin(ang).T
    rope[1, 16:32] = np.sin(ang).T
    perm = np.zeros((128, 128), np.float32)
    for m in range(16):
        perm[m + 16, m] = 1.0
        perm[m, m + 16] = 1.0
    am = np.zeros((4, 128, 512), np.float32)
    kk = np.arange(128)[:, None]
    qq = np.arange(128)[None, :]
    diag = ((kk // 64) <= (qq // 64)).astype(np.float32)
    for i in range(4):
        for j in range(4):
            if j > i:
                am[i][:, j * 128:(j + 1) * 128] = 1.0
            elif j == i:
                am[i][:, j * 128:(j + 1) * 128] = diag
    pb = np.zeros((128, 1), np.float32)
    pb[:NPAD] = -30000.0
    return rope, perm, am, pb


def _rwkv_consts():
    cm = np.zeros((6, 128, 128), np.float32)
    cm[0] = np.eye(128)
    r = np.arange(128)[:, None]
    c = np.arange(128)[None, :]
    same = (r // 64) == (c // 64)
    cm[1] = same
    cm[2] = same & (r < c)
    cm[3] = same & (r <= c)
    cm[4] = same & (c < r)
    return cm


def build_layerA(kind, layer_idx, j, Lp=None, D=D_MODEL):
    Lp = Lp or LP
    nc = bass.Bass("TRN2", target_bir_lowering=False)
    dt = lambda name, shape, k="ExternalInput", t=F32: nc.dram_tensor(name, shape, t, kind=k).ap()
    h = dt("h", [Lp, D])
    g = dt("g", [1, D])
    ident = dt("ident", [128, 128])
    og = dt("og", [Lp, 512], "ExternalOutput")
    uT_scr = dt("uT_scr", [128, D // 128, Lp + 1], "Internal", BF16)
    W1 = dt("W1", [D, 1024])
    W2 = dt("W2", [D, 1024])
    nzf = 1280 if kind == 'rwkv' else 1024
    zF = dt("zF", [nzf, Lp], "Internal")
    zT = dt("zT", [Lp, 1024], "Internal")
    S = Sched(nc)
    emit_prep(S, nc, Lp=Lp, D=D, h=h, g=g, uT_scr=uT_scr, ident=ident)
    if kind == 'rwkv':
        mu = dt("mu", [6, D])
        W3 = dt("W3", [D, 256])
        t1 = [('F', 0, c * 128, c * 128) for c in range(4)] + [('F', 1, 512 + c * 128, 512 + c * 128) for c in range(4)]
        emit_mm_pass(S, nc, Lp=Lp, D=D, uT_scr=uT_scr, W=W1, ncols=1024, tiles=t1, zF=zF, zT=zT, mu=mu, nvar=6)
        t2 = [('T', 2, 0, 0), ('T', 3, 512, 512)]
        emit_mm_pass(S, nc, Lp=Lp, D=D, uT_scr=uT_scr, W=W2, ncols=1024, tiles=t2, zF=zF, zT=zT, mu=mu, nvar=6)
        t3 = [('F', 4, 0, 1024), ('F', 5, 128, 1152)]
        emit_mm_pass(S, nc, Lp=Lp, D=D, uT_scr=uT_scr, W=W3, ncols=256, tiles=t3, zF=zF, zT=zT, mu=mu, nvar=6)
        w2 = dt("w2", [128, 512])
        a2 = dt("a2", [128, 512])
        vecs = dt("vecs", [128, 5, 4])
        lnx = dt("lnx", [2, 512])
        cmat = dt("cmat", [6, 128, 128])
        emit_rwkv(S, nc, Lp=Lp, zF=zF, zT=zT, w2=w2, a2=a2, vecs=vecs, lnx=lnx, og=og, cmat=cmat)
    else:
        t1 = [('F', -1, c * 128, c * 128) for c in range(8)]
        emit_mm_pass(S, nc, Lp=Lp, D=D, uT_scr=uT_scr, W=W1, ncols=1024, tiles=t1, zF=zF, zT=zT)
        t2 = [('T', -1, 0, 0), ('T', -1, 512, 512)]
        emit_mm_pass(S, nc, Lp=Lp, D=D, uT_scr=uT_scr, W=W2, ncols=1024, tiles=t2, zF=zF, zT=zT)
        if kind == 'hgrn2':
            lbl = dt("lbl", [128, 2, 4])
            gn = dt("gn", [1, 128])
            masks = dt("masks", [3, 128, 512])
            emit_hgrn2(S, nc, Lp=Lp, zF=zF, zT=zT, lbl=lbl, n_la=2, layer_j=j, gn=gn, og=og, ident=ident, masks=masks, NH=4)
        else:
            lamv = dt("lamv", [4, 128])
            sublng = dt("sublng", [1, 256])
            rope = dt("rope", [2, 128, Lp])
            perm = dt("perm", [128, 128])
            amask = dt("amask", [4, 128, 512])
            padbias = dt("padbias", [128, 1])
            lambda_init = 0.8 - 0.6 * float(np.exp(-0.3 * layer_idx))
            emit_attn(S, nc, Lp=Lp, zF=zF, zT=zT, lamv=lamv, lambda_init=lambda_init, sublng=sublng, og=og, rope=rope, perm=perm,
                      amask=amask, padbias=padbias, NH=2, npad=NPAD)
    S.finish('sp')
    S.close()
    return nc


def build_layerB(NT=None, D=D_MODEL):
    NT = NT or NTB
    nc = bass.Bass("TRN2", target_bir_lowering=False)
    dt = lambda name, shape, k="ExternalInput", t=F32: nc.dram_tensor(name, shape, t, kind=k).ap()
    L = NT * 128
    ogc = dt("ogc", [L, D])
    hc = dt("hc", [L, D])
    g = dt("g", [1, D])
    ident = dt("ident", [128, 128])
    Wo = [dt("Wo%d" % i, [D, 1024]) for i in range(4)]
    hn = dt("hn", [L, D], "ExternalOutput")
    oT_scr = dt("oT_scr", [128, D // 128, L + 1], "Internal", BF16)
    y_scr = dt("y_scr", [L, D], "Internal")
    zdummy = dt("zdummy", [128, 128], "Internal")
    S = Sched(nc)
    emit_prep(S, nc, Lp=L, D=D, h=ogc, g=g, uT_scr=oT_scr, ident=ident, norm=False)
    for i in range(4):
        tl = [('T', -1, 0, i * 1024), ('T', -1, 512, i * 1024 + 512)]
        emit_mm_pass(S, nc, Lp=L, D=D, uT_scr=oT_scr, W=Wo[i], ncols=1024, tiles=tl, zF=zdummy, zT=y_scr)
    emit_postnorm(S, nc, NT=NT, D=D, y=y_scr, h=hc, g=g, out=hn)
    S.finish('sp')
    S.close()
    return nc


_NC_CACHE = {}


def _get_nc(key, fn):
    if key not in _NC_CACHE:
        _NC_CACHE[key] = fn()
    return _NC_CACHE[key]


def _f32(a):
    return np.ascontiguousarray(a, dtype=np.float32)


def _layerA_inmaps(kind, layer_idx, j, h, P):
    maps = []
    g = _f32(P['pre_norm_g'][layer_idx][None, :])
    ident = np.eye(128, dtype=np.float32)
    d = D_MODEL
    if kind == 'attn':
        rope, perm, am, pb = _attn_consts(LP)
    for c in range(NCORES):
        cs = slice(c * 512, (c + 1) * 512)
        m = {"h": h, "g": g, "ident": ident}
        if kind == 'hgrn2':
            w = P['a_w_in'][j]
            m["W1"] = _f32(np.concatenate([w[:, cs], w[:, d + c * 512:d + (c + 1) * 512]], 1))
            m["W2"] = _f32(np.concatenate([w[:, 2 * d + c * 512:2 * d + (c + 1) * 512], w[:, 3 * d + c * 512:3 * d + (c + 1) * 512]], 1))
            lg = P['a_lb_logits'][:, cs]
            m["lbl"] = _f32(lg.reshape(2, 4, 128).transpose(2, 0, 1))
            m["gn"] = _f32(P['a_gnorm_g'][j][None, :])
            m["masks"] = _hgrn_masks()
        elif kind == 'attn':
            w = P['b_w_in'][j]
            m["W1"] = _f32(np.concatenate([w[:, cs], w[:, d + c * 512:d + (c + 1) * 512]], 1))
            m["W2"] = _f32(np.concatenate([w[:, 2 * d + c * 512:2 * d + (c + 1) * 512], w[:, 3 * d + c * 512:3 * d + (c + 1) * 512]], 1))
            m["lamv"] = _f32(np.stack([P['b_lam_q1'][j], P['b_lam_k1'][j], P['b_lam_q2'][j], P['b_lam_k2'][j]]))
            m["sublng"] = _f32(P['b_subln_g'][j][None, :])
            m["rope"] = rope
            m["perm"] = perm
            m["amask"] = am
            m["padbias"] = pb
        else:
            w = P['c_w_in'][j]
            m["W1"] = _f32(np.concatenate([w[:, cs], w[:, d + c * 512:d + (c + 1) * 512]], 1))
            m["W2"] = _f32(np.concatenate([w[:, 2 * d + c * 512:2 * d + (c + 1) * 512], w[:, 3 * d + c * 512:3 * d + (c + 1) * 512]], 1))
            m["W3"] = _f32(np.concatenate([P['c_w1'][j], P['c_a1'][j]], 1))
            m["mu"] = _f32(P['c_mu'][j])
            m["w2"] = _f32(P['c_w2'][j][:, cs])
            m["a2"] = _f32(P['c_a2'][j][:, cs])
            vs = [P['c_w0'][j], P['c_a0'][j], P['c_k_k'][j], P['c_k_a'][j], P['c_r_k'][j].reshape(-1)]
            m["vecs"] = _f32(np.stack([v[cs].reshape(4, 128).T for v in vs], 1))
            m["lnx"] = _f32(np.stack([P['c_lnx_w'][j][cs], P['c_lnx_b'][j][cs]]))
            m["cmat"] = _rwkv_consts()
        maps.append(m)
    return maps


def _tok_rows(c):
    per = (LP - 128) // NCORES
    return np.concatenate([np.arange(0, 128), np.arange(128 + c * per, 128 + (c + 1) * per)])


def kernel(**inputs):
    P = {k: np.asarray(v) for k, v in inputs.items()}
    x = P['x']
    h = np.zeros((LP, D_MODEL), np.float32)
    h[NPAD:NPAD + N_META] = P['meta_tokens']
    h[NPAD + N_META:] = x[0]
    kinds = ['hgrn2', 'attn', 'rwkv']
    ident = np.eye(128, dtype=np.float32)
    for i in range(4):
        kind = kinds[i % 3]
        j = i // 3
        ncA = _get_nc(('A', kind, i, j), lambda: build_layerA(kind, i, j))
        resA = run_bass_kernel_spmd(ncA, _layerA_inmaps(kind, i, j, h, P), core_ids=list(range(NCORES)))
        og = np.concatenate([r["og"] for r in resA.results], axis=1)
        ncB = _get_nc(('B',), build_layerB)
        wo = {'hgrn2': P['a_w_out'], 'attn': P['b_w_out'], 'rwkv': P['c_w_out']}[kind][j]
        wos = {"Wo%d" % q: _f32(wo[:, q * 1024:(q + 1) * 1024]) for q in range(4)}
        gB = _f32(P['post_norm_g'][i][None, :])
        mapsB = []
        for c in range(NCORES):
            rows = _tok_rows(c)
            m = {"ogc": _f32(og[rows]), "hc": _f32(h[rows]), "g": gB, "ident": ident}
            m.update(wos)
            mapsB.append(m)
        resB = run_bass_kernel_spmd(ncB, mapsB, core_ids=list(range(NCORES)))
        hn = np.empty_like(h)
        hn[0:128] = resB.results[0]["hn"][0:128]
        per = (LP - 128) // NCORES
        for c in range(NCORES):
            hn[128 + c * per:128 + (c + 1) * per] = resB.results[c]["hn"][128:]
        h = hn
    return np.ascontiguousarray(h[NPAD + N_META:][None]).astype(np.float32)
```
